# Optimizing a Trainium2 kernel written in Bass

```python
import math
import jax, jax.numpy as jnp
from jax import lax
import numpy as np

D_MODEL = 1024
BATCH = 8
SEQ = 2048
DEPTH = 4
DEC_BATCH = 128
DEC_SEQ = 1
PAST_LEN = 16384
PAGE_SIZE = 128

A_WIDTH = D_MODEL
A_GROUPS = 8
A_GROUP_DIM = A_WIDTH // A_GROUPS
A_CHUNK = 128
B_HEADS = 8
B_HEAD_DIM = D_MODEL // B_HEADS
B_KEY = B_HEADS * B_HEAD_DIM
B_VAL = B_HEADS * B_HEAD_DIM
CONV_W = 4
CONV_DIM = 2 * B_KEY + B_VAL
B_CHUNK = 64
IN_SIZES = (A_WIDTH, A_WIDTH, A_WIDTH, CONV_DIM, B_VAL, B_HEADS, B_HEADS, D_MODEL, D_MODEL)
IN_DIM = 3 * A_WIDTH + CONV_DIM + B_VAL + 2 * B_HEADS + 2 * D_MODEL
EPS = 1e-6

kernel_name = "hybrid_gmlp_gated_deltanet_decode_step"


def rmsnorm(x, g):
    xf = x.astype(jnp.float32)
    y = xf * lax.rsqrt(jnp.mean(xf * xf, axis=-1, keepdims=True) + EPS)
    return (y * g.astype(jnp.float32)).astype(x.dtype)


def l2norm(x):
    return x * lax.rsqrt(jnp.sum(x * x, axis=-1, keepdims=True) + EPS)


def chunk_spatial_gating(u, v, z, norm_g, w_s, b_s):
    bsz, seq, _ = v.shape
    lc = min(A_CHUNK, seq)
    pad = (-seq) % lc
    vn = rmsnorm(v, norm_g)
    vp = jnp.pad(vn, ((0, 0), (0, pad), (0, 0))) if pad else vn
    n = (seq + pad) // lc
    vc = vp.reshape(bsz, n, lc, A_GROUPS, A_GROUP_DIM)
    causal = jnp.tril(jnp.ones((lc, lc), dtype=bool))
    ws = jnp.where(causal, w_s[:, :lc, :lc], 0).astype(v.dtype)
    bias = b_s[:, :lc].T[None, None, :, :, None].astype(v.dtype)
    mixed = jnp.einsum('gts,bnsgc->bntgc', ws, vc) + bias
    mixed = mixed.reshape(bsz, seq + pad, A_WIDTH)[:, :seq]
    return u * mixed * jax.nn.silu(z), vn


def short_conv(x, buf, w):
    seq = x.shape[1]
    xe = jnp.concatenate([buf.astype(x.dtype), x], axis=1)
    out = sum(xe[:, i:i + seq] * w[i] for i in range(CONV_W))
    return jax.nn.silu(out), xe[:, -(CONV_W - 1):]


def to_chunks(t, n, c):
    b, _, h = t.shape[:3]
    t = t.reshape((b, n, c, h) + t.shape[3:])
    return jnp.moveaxis(t, 3, 1)


def gated_delta(q, k, v, beta, g, s0):
    bsz, seq, nh, _ = q.shape
    dv = v.shape[-1]
    c = min(B_CHUNK, seq)
    pad = (-seq) % c
    if pad:
        padw = lambda t: jnp.pad(t, ((0, 0), (0, pad)) + ((0, 0),) * (t.ndim - 2))
        q, k, v, beta, g = (padw(t) for t in (q, k, v, beta, g))
    n = (seq + pad) // c
    q, k, v, beta, g = (to_chunks(t, n, c) for t in (q, k, v, beta, g))
    gc = jnp.cumsum(g, axis=-1)
    incl = jnp.tril(jnp.ones((c, c), dtype=bool))
    strict = jnp.tril(jnp.ones((c, c), dtype=bool), -1)
    decay = jnp.exp(jnp.where(incl, gc[..., :, None] - gc[..., None, :], -jnp.inf))
    kk = jnp.einsum('bhnid,bhnjd->bhnij', k, k)
    t_mat = jnp.where(strict, beta[..., :, None] * kk * decay, 0.0) + jnp.eye(c, dtype=jnp.float32)
    rhs = jnp.concatenate([v * beta[..., None], k * (beta * jnp.exp(gc))[..., None]], axis=-1)
    sol = lax.linalg.triangular_solve(t_mat, rhs, left_side=True, lower=True, unit_diagonal=True)
    u, w = sol[..., :dv], sol[..., dv:]
    attn = jnp.einsum('bhnid,bhnjd->bhnij', q, k) * decay
    q_dec = q * jnp.exp(gc)[..., None]
    k_dec = k * jnp.exp(gc[..., -1:] - gc)[..., None]
    g_last = jnp.exp(gc[..., -1])

    def step(s, xs):
        u_c, w_c, qd_c, kd_c, a_c, gl_c = xs
        v_new = u_c - jnp.einsum('bhck,bhkv->bhcv', w_c, s)
        o_c = jnp.einsum('bhck,bhkv->bhcv', qd_c, s) + jnp.einsum('bhij,bhjv->bhiv', a_c, v_new)
        s = s * gl_c[..., None, None] + jnp.einsum('bhck,bhcv->bhkv', kd_c, v_new)
        return s, o_c

    xs = tuple(jnp.moveaxis(t, 2, 0) for t in (u, w, q_dec, k_dec, attn, g_last))
    s_fin, o = lax.scan(step, s0, xs)
    o = o.transpose(1, 0, 3, 2, 4).reshape(bsz, n * c, nh, dv)[:, :seq]
    return o, s_fin


def mixer_layer(x, conv_buf, s0, pre_g, w_in, gmlp_g, w_s, b_s, conv_w, a_log, dt_bias, gdn_g, w_pa, w_pb, w_out, post_g):
    bsz, seq, _ = x.shape
    h = rmsnorm(x, pre_g)
    proj = h @ w_in
    splits = np.cumsum(IN_SIZES)[:-1].tolist()
    u_a, v_a, z_a, qkv, z_b, a_b, b_b, gate_a, gate_b = jnp.split(proj, splits, axis=-1)
    y_a, v_rows = chunk_spatial_gating(u_a, v_a, z_a, gmlp_g, w_s, b_s)
    qkv, new_buf = short_conv(qkv, conv_buf, conv_w)
    qkv = qkv.astype(jnp.float32)
    hs = (bsz, seq, B_HEADS, B_HEAD_DIM)
    q = l2norm(qkv[..., :B_KEY].reshape(hs)) * (B_HEAD_DIM ** -0.5)
    k = l2norm(qkv[..., B_KEY:2 * B_KEY].reshape(hs))
    v = qkv[..., 2 * B_KEY:].reshape(hs)
    beta = jax.nn.sigmoid(b_b.astype(jnp.float32))
    g = -jnp.exp(a_log.astype(jnp.float32)) * jax.nn.softplus(a_b.astype(jnp.float32) + dt_bias.astype(jnp.float32))
    o, s_new = gated_delta(q, k, v, beta, g, s0.astype(jnp.float32))
    o = rmsnorm(o, gdn_g) * jax.nn.silu(z_b.astype(jnp.float32).reshape(hs))
    y_b = o.reshape(bsz, seq, B_VAL).astype(x.dtype)
    merged = jax.nn.sigmoid(gate_a) * (y_a @ w_pa) + jax.nn.sigmoid(gate_b) * (y_b @ w_pb)
    out = merged @ w_out
    return x + rmsnorm(out, post_g), new_buf, s_new, v_rows


def setup_inputs(seed: int = 0) -> dict:
    key = jax.random.key(seed)
    ks = jax.random.split(key, 17)
    nrm = lambda k, shape, scale: jax.random.normal(k, shape, jnp.float32) * scale
    dt = jnp.exp(jax.random.uniform(ks[11], (DEPTH, B_HEADS), jnp.float32, math.log(1e-3), math.log(1e-1)))
    return {
        "x_prompt": nrm(ks[0], (BATCH, SEQ, D_MODEL), 1.0),
        "x_sample": nrm(ks[1], (DEC_BATCH, DEC_SEQ, D_MODEL), 1.0),
        "state_gdn": nrm(ks[2], (DEPTH, DEC_BATCH, B_HEADS, B_HEAD_DIM, B_HEAD_DIM), 0.1),
        "state_conv": nrm(ks[3], (DEPTH, DEC_BATCH, CONV_W - 1, CONV_DIM), 1.0),
        "pre_norm": 1.0 + nrm(ks[4], (DEPTH, D_MODEL), 0.02),
        "w_in": nrm(ks[5], (DEPTH, D_MODEL, IN_DIM), D_MODEL ** -0.5),
        "gmlp_norm": 1.0 + nrm(ks[6], (DEPTH, A_WIDTH), 0.02),
        "w_spatial": nrm(ks[7], (DEPTH, A_GROUPS, A_CHUNK, A_CHUNK), A_CHUNK ** -0.5),
        "b_spatial": 1.0 + nrm(ks[8], (DEPTH, A_GROUPS, A_CHUNK), 0.02),
        "conv_w": nrm(ks[9], (DEPTH, CONV_W, CONV_DIM), CONV_W ** -0.5),
        "a_log": jnp.log(jax.random.uniform(ks[10], (DEPTH, B_HEADS), jnp.float32, 1.0, 16.0)),
        "dt_bias": dt + jnp.log(-jnp.expm1(-dt)),
        "gdn_norm": 1.0 + nrm(ks[12], (DEPTH, B_HEAD_DIM), 0.02),
        "w_proj_a": nrm(ks[13], (DEPTH, A_WIDTH, D_MODEL), A_WIDTH ** -0.5),
        "w_proj_b": nrm(ks[14], (DEPTH, B_VAL, D_MODEL), B_VAL ** -0.5),
        "w_out": nrm(ks[15], (DEPTH, D_MODEL, D_MODEL), D_MODEL ** -0.5),
        "post_norm": 1.0 + nrm(ks[16], (DEPTH, D_MODEL), 0.02),
    }


def reference(x_prompt, x_sample, state_gdn, state_conv, pre_norm, w_in, gmlp_norm, w_spatial, b_spatial, conv_w, a_log, dt_bias, gdn_norm, w_proj_a, w_proj_b, w_out, post_norm):
    bp = x_prompt.shape[0]
    conv_zero = jnp.zeros((bp, CONV_W - 1, CONV_DIM), x_prompt.dtype)
    s_zero = jnp.zeros((bp, B_HEADS, B_HEAD_DIM, B_HEAD_DIM), jnp.float32)
    xp, xs = x_prompt, x_sample
    gdn_p, conv_p, gdn_s, conv_s, vrows_s = [], [], [], [], []
    for l in range(DEPTH):
        w = (pre_norm[l], w_in[l], gmlp_norm[l], w_spatial[l], b_spatial[l], conv_w[l], a_log[l], dt_bias[l],
             gdn_norm[l], w_proj_a[l], w_proj_b[l], w_out[l], post_norm[l])
        xp, cb_p, sg_p, _ = mixer_layer(xp, conv_zero, s_zero, *w)
        xs, cb_s, sg_s, vr_s = mixer_layer(xs, state_conv[l], state_gdn[l], *w)
        gdn_p.append(sg_p)
        conv_p.append(cb_p)
        gdn_s.append(sg_s)
        conv_s.append(cb_s)
        vrows_s.append(vr_s)
    return (xp, xs, jnp.stack(gdn_p), jnp.stack(conv_p), jnp.stack(gdn_s), jnp.stack(conv_s), jnp.stack(vrows_s))
```

```python
import threading
import numpy as np
from contextlib import ExitStack
import concourse.bass as bass
import concourse.mybir as mybir
from concourse.bass_utils import run_bass_kernel_spmd

F32 = mybir.dt.float32
BF16 = mybir.dt.bfloat16
F32R = mybir.dt.float32r
AF = mybir.ActivationFunctionType
ALU = mybir.AluOpType

D = 1024
KC = 8
TB = 512
NBLK = 4
NS = 16
L = 4
H = 8
TOK = TB + NS
IN_DIM = 9232
C_U, C_V, C_Z, C_Q, C_K, C_VV, C_ZB, C_A, C_GA, C_GB = 0, 1024, 2048, 3072, 4096, 5120, 6144, 7168, 7184, 8208
EPS = 1e-6
NSLOT = 4
NRING = 24
ENGS = ('pe', 'act', 'dve', 'pool', 'sp')
_TL = threading.local()


class Tok:
    __slots__ = ('w', 'r', 'al')

    def __init__(self):
        self.w = None
        self.r = {}
        self.al = []


def toks(n):
    return [Tok() for _ in range(n)]


class Prog:
    def __init__(self):
        self.q = {e: [] for e in ENGS}
        self.cnt = {e: 0 for e in ENGS}
        self.waited = {e: {} for e in ENGS}
        self.dma_idx = {'sp': 0, 'pool': 0}
        self.dma_uses = {}
        self.stopped = False
        self.tick = None

    def _deps(self, reads, writes):
        deps = {}

        def add(k, v):
            if deps.get(k, 0) < v:
                deps[k] = v

        for t in reads:
            for a in [t] + t.al:
                if a.w is not None:
                    add(*a.w)
        for t in writes:
            for a in [t] + t.al:
                if a.w is not None:
                    add(*a.w)
                for k, v in a.r.items():
                    add(k, v)
        return deps

    def _waits(self, eng, deps):
        waits = []
        for k, v in deps.items():
            if k == eng and eng == 'pe':
                continue
            if self.waited[eng].get(k, 0) >= v:
                continue
            self.waited[eng][k] = v
            waits.append((k, v))
        return waits

    def emit(self, eng, fn, reads=(), writes=(), signal=True):
        if self.stopped:
            return
        deps = self._deps(reads, writes)
        waits = self._waits(eng, deps)
        if signal:
            self.cnt[eng] += 1
            n = self.cnt[eng]
        else:
            n = self.cnt[eng] + 1
        self.q[eng].append((waits, fn, eng if signal else None))
        for t in reads:
            if t.r.get(eng, 0) < n:
                t.r[eng] = n
        for t in writes:
            t.w = (eng, n)
            t.r = {}
        if self.tick is not None:
            self.tick()

    def dma(self, queue, fn, reads=(), writes=()):
        if self.stopped:
            return
        slot = self.dma_idx[queue] % NRING
        self.dma_idx[queue] += 1
        key = (queue, slot)
        uses = self.dma_uses.get(key, 0)
        deps = self._deps(reads, writes)
        if uses > 0:
            deps[key] = max(deps.get(key, 0), 16 * uses)
        waits = self._waits(queue, deps)
        val = 16 * (uses + 1)
        self.dma_uses[key] = uses + 1
        self.q[queue].append((waits, fn, key))
        for t in reads:
            t.r[key] = val
        for t in writes:
            t.w = (key, val)
            t.r = {}
        if self.tick is not None:
            self.tick()


def co_run(P, tasks):
    if len(tasks) == 1:
        tasks[0][0]()
        return
    n = len(tasks)
    st = {'turn': 0, 'done': [False] * n, 'err': None}
    cv = threading.Condition()
    tl = threading.local()
    left = [t[1] for t in tasks]

    def nxt(i):
        for k in range(1, n + 1):
            j = (i + k) % n
            if not st['done'][j]:
                return j
        return -1

    def tick():
        i = tl.idx
        left[i] -= 1
        if left[i] > 0:
            return
        left[i] = tasks[i][1]
        with cv:
            j = nxt(i)
            if j == i or j < 0:
                return
            st['turn'] = j
            cv.notify_all()
            while st['turn'] != i:
                cv.wait()

    def worker(i):
        tl.idx = i
        _TL.bank_subset = tasks[i][2] if len(tasks[i]) > 2 else None
        _TL.bank_i = 0
        with cv:
            while st['turn'] != i:
                cv.wait()
        try:
            tasks[i][0]()
        except BaseException as e:
            st['err'] = e
        finally:
            with cv:
                st['done'][i] = True
                st['turn'] = nxt(i)
                cv.notify_all()

    P.tick = tick
    ths = [threading.Thread(target=worker, args=(i,)) for i in range(n)]
    for t in ths:
        t.start()
    for t in ths:
        t.join()
    P.tick = None
    if st['err'] is not None:
        raise st['err']


def act_fn(out, in_, func, **kw):
    return lambda e: e.activation(out=out, in_=in_, func=func, **kw)


def tt_fn(out, a, b, op):
    return lambda e: e.tensor_tensor(out=out, in0=a, in1=b, op=op)


def ts_fn(out, a, s1, s2, op0, op1=None):
    if op1 is None:
        return lambda e: e.tensor_scalar(out=out, in0=a, scalar1=s1, scalar2=None, op0=op0)
    return lambda e: e.tensor_scalar(out=out, in0=a, scalar1=s1, scalar2=s2, op0=op0, op1=op1)


def stt_fn(out, a, s, b, op0, op1):
    return lambda e: e.scalar_tensor_tensor(out=out, in0=a, scalar=s, in1=b, op0=op0, op1=op1)


def cp_fn(out, in_):
    return lambda e: e.tensor_copy(out=out, in_=in_)


def rcp_fn(out, in_):
    return lambda e: e.reciprocal(out=out, in_=in_)


def mm_fn(out, l, r, st, sp):
    return lambda e: e.matmul(out, l, r, start=st, stop=sp)


def dma_fn(out, in_, **kw):
    return lambda e: e.dma_start(out=out, in_=in_, **kw)


class StopEmit(Exception):
    pass


def build(depth=L, nblk=NBLK, stop=99):
    nc = bass.Bass("TRN2", target_bir_lowering=False)
    P = Prog()
    es = ExitStack()

    def din(name, shape):
        return nc.dram_tensor(name, shape, F32, kind="ExternalInput").ap()

    def dout(name, shape):
        return nc.dram_tensor(name, shape, F32, kind="ExternalOutput").ap()

    x_p = din("x_p", [2048, D])
    x_s = din("x_s", [NS, D])
    sg = din("sg", [L, NS, H, 128, 128])
    sc = din("sc", [L, NS, 3, 3072])
    pre_norm = din("pre_norm", [L, D])
    w_in = din("w_in", [L, D, IN_DIM])
    gmlp_norm = din("gmlp_norm", [L, D])
    w_spatial = din("w_spatial", [L, 8, 128, 128])
    b_spatial = din("b_spatial", [L, 8, 128])
    conv_w = din("conv_w", [L, 4, 3072])
    a_log = din("a_log", [L, H])
    dt_bias = din("dt_bias", [L, H])
    gdn_norm = din("gdn_norm", [L, 128])
    w_proj_a = din("w_proj_a", [L, D, D])
    w_proj_b = din("w_proj_b", [L, D, D])
    w_out = din("w_out", [L, D, D])
    post_norm = din("post_norm", [L, D])
    consts = din("consts", [128, 6 * 128])

    y_p = dout("y_p", [2048, D])
    y_s = dout("y_s", [NS, D])
    gdn_p = dout("gdn_p", [L, H, 128, 128])
    conv_p = dout("conv_p", [L, 3, 3072])
    gdn_s = dout("gdn_s", [L, NS, H, 128, 128])
    conv_s = dout("conv_s", [L, NS, 3, 3072])
    vrow_s = dout("vrow_s", [L, NS, D])

    def sb(name, shape, dtype=F32):
        return es.enter_context(nc.sbuf_tensor(name, shape, dtype))

    x_tm = sb("x_tm", [128, 17, D])
    wsl = sb("wsl", [128, NSLOT, KC, 512], BF16)
    hT = sb("hT", [128, KC, TOK], BF16)
    yaT = sb("yaT", [128, KC, TOK], BF16)
    ybT = sb("ybT", [128, KC, TOK], BF16)
    mg = sb("mg", [128, KC, TOK], BF16)
    vn_s = sb("vn_s", [128, D], BF16)
    gslot = sb("gslot", [128, D])
    gmlp_b = gslot
    bsp_b = sb("bsp_b", [128, 2, 128])
    wsT = sb("wsT", [128, 8, 128], BF16)
    gdn_b = sb("gdn_b", [128, 128])
    smallp = sb("smallp", [128, 64])
    convw = sb("convw", [128, 1, 24, 4])
    cst = sb("cst", [128, 6, 128])
    cst_bf = sb("cst_bf", [128, 2, 128], BF16)
    epsb = sb("epsb", [128, 1])
    idr = sb("idr", [128, 128], F32R)
    tmpA = sb("tmpA", [128, 2 * TOK])
    jh = sb("jh", [128, D], BF16)
    pre = sb("pre", [128, 1, 3 + TB])
    tails = sb("tails", [128, 24, 3])
    acc = tmpA[:, 0:TOK]
    sq = jh[:, 0:TOK]
    rb = tmpA[:, TOK:2 * TOK]
    qn2 = sb("qn2", [128, 3, TOK], BF16)
    kn2 = sb("kn2", [128, 3, TOK], BF16)
    vT2 = sb("vT2", [128, 3, TOK], BF16)
    szb2 = sb("szb2", [128, 3, TOK])
    S = sb("S", [128, H, 128])
    S_bf = sb("S_bf", [128, H, 128], BF16)
    scal = sb("scal", [128, 16, 16])
    st0 = sb("st0", [128, 8])
    dl = sb("dl", [128, 3, 128])
    sst_in = sb("sst_in", [128, 4, 128])
    sst_out = sb("sst_out", [128, 2, 128])
    cstage = sb("cstage", [128, 3, 128])
    stt = sb("stt", [128, 8, NS])
    stm = sb("stm", [128, 4, 128])
    stmb = sb("stmb", [128, 2, 128], BF16)
    spad = sb("spad", [128, 2, 128])
    nm = sb("nm", [128, 1, 4, 128])
    nmr = sb("nmr", [128, 3, 4, 128], F32R)
    vt2 = sb("vt2", [128, 2, 4, 128])
    dlb2 = sb("dlb2", [128, 2, 12, 128], BF16)
    dlc = sb("dlc", [128, 3, 128], BF16)
    sc2 = sb("sc2", [128, 4, 5, 8])

    banks = [es.enter_context(nc.psum_tensor("ps%d" % i, [128, 512], F32)) for i in range(8)]
    bank_t = toks(8)
    bank_i = [0]

    def bank():
        sub = getattr(_TL, 'bank_subset', None)
        if sub:
            i = sub[_TL.bank_i % len(sub)]
            _TL.bank_i += 1
        else:
            i = bank_i[0] % 8
            bank_i[0] += 1
        return banks[i], bank_t[i]

    x_t = toks(17)
    w_t = [toks(4) for _ in range(NSLOT)]
    hT_t = toks(5)
    ya_t = toks(9)
    yb_t = toks(9)
    mg_t = toks(8)
    vn_t = toks(4)
    for a in vn_t:
        a.al = list(mg_t)
    for a in mg_t:
        a.al = list(vn_t)
    mg_s_t = Tok()
    mg_s_t.al = list(vn_t)
    for a in vn_t:
        a.al.append(mg_s_t)
    vns_t, gslot_t, wsT_t, gdn_t, smallp_t, convw_t, cst_t = toks(7)
    bsp2_t = toks(2)
    gmlp_t = gslot_t
    tmpA_t, jh_t, tails_t = toks(3)
    acc_t = tmpA_t
    sq_t = jh_t
    rb_t = tmpA_t
    qn2_t, kn2_t, vT2_t, szb2_t = toks(3), toks(3), toks(3), toks(3)
    pre_t = toks(2)
    pres_t = toks(2)
    S_t = toks(H)
    Sb_t = toks(H)
    scal_t = toks(16)
    st0_t = Tok()
    st0s_t = Tok()
    dl_t = toks(8)
    ssi_t = toks(4)
    sso_t = toks(2)
    cstage_t = Tok()
    stt_t = toks(8)
    stm_t = toks(4)
    stmb_t = toks(2)
    spad_t = toks(2)
    nm_t = toks(16)
    vt2_t = [toks(4), toks(4)]
    dlb2_t = [toks(12), toks(12)]
    dlc_t = toks(3)
    sc2_t = toks(4)

    vn = mg[:].rearrange("p a b -> p (a b)")[:, 0:4096].rearrange("p (a b) -> p a b", a=4)
    I_f = cst[:, 0, :]
    U_f = cst[:, 1, :]
    SU_f = cst[:, 2, :]
    NM_f = cst[:, 3, :]
    ONES_f = cst[:, 4, :]
    TRIL_f = cst[:, 5, :]
    I_b = cst_bf[:, 0, :]
    ONES_b = cst_bf[:, 1, :]

    ACT = lambda f, r, w: P.emit('act', f, r, w)
    DVE = lambda f, r, w: P.emit('dve', f, r, w)

    def MM(out, l, r, st, sp, reads, writes):
        P.emit('pe', mm_fn(out, l, r, st, sp), reads, writes, signal=sp)

    def LD(out, in_, reads, writes, **kw):
        P.dma('sp', dma_fn(out, in_, **kw), reads, writes)

    LD(cst[:], consts.rearrange("p (a b) -> p a b", a=6), [], [cst_t])
    DVE(cp_fn(cst_bf[:, 0, :], cst[:, 0, :]), [cst_t], [cst_t])
    DVE(cp_fn(cst_bf[:, 1, :], cst[:, 4, :]), [cst_t], [cst_t])
    DVE(lambda e: e.memset(epsb[:], EPS), [], [cst_t])
    DVE(cp_fn(idr[:], cst[:, 0, :]), [cst_t], [cst_t])
    DVE(lambda e: e.memset(spad[:], 0.0), [], spad_t)
    DVE(lambda e: e.memset(cstage[:], 0.0), [], [cstage_t])
    for ti in range(16):
        LD(x_tm[:, ti, :], x_p[ti * 128:(ti + 1) * 128, :], [], [x_t[ti]])
    LD(x_tm[0:NS, 16, :], x_s[:, :], [], [x_t[16]])

    wi = [0]

    def wview(wap, l):
        return wap[l].rearrange("(kc p) n -> p kc n", p=128)

    def load_w(parts):
        s = wi[0] % NSLOT
        wi[0] += 1
        for (view, c0, ncols, dc) in parts:
            tk = [w_t[s][qq] for qq in range(4) if dc < (qq + 1) * 128 and dc + ncols > qq * 128]
            P.dma('pool', dma_fn(wsl[:, s, :, dc:dc + ncols], view[:, :, c0:c0 + ncols]), [], tk)
        return wsl[:, s], w_t[s]

    def rsqrt_col(dst, src, n, rows, scale, rt, wt):
        ACT(act_fn(dst, src, AF.Ln, bias=epsb[0:rows, 0:1], scale=scale), rt + [cst_t], wt)
        ACT(act_fn(dst, dst, AF.Exp, scale=-0.5), wt, wt)

    def ck(level):
        if stop <= level:
            P.stopped = True

    for l in range(depth if stop > 0 else 0):
        win_v = wview(w_in, l)
        wpa_v = wview(w_proj_a, l)
        wpb_v = wview(w_proj_b, l)
        wo_v = wview(w_out, l)
        for i in range(4):
            LD(convw[:, 0, :, i], conv_w[l, i, :].rearrange("(c p) -> p c", p=128), [], [convw_t],
               allow_slow_non_contiguous=True)
        LD(gdn_b[:], gdn_norm[l].partition_broadcast(128), [], [gdn_t])
        LD(smallp[:, 0:8], a_log[l].partition_broadcast(128), [], [smallp_t])
        LD(smallp[:, 8:16], dt_bias[l].partition_broadcast(128), [], [smallp_t])
        LD(smallp[:, 24:32], w_spatial[l, :, 0, 0].partition_broadcast(128), [], [smallp_t], allow_slow_non_contiguous=True)
        LD(smallp[:, 32:40], b_spatial[l, :, 0].partition_broadcast(128), [], [smallp_t], allow_slow_non_contiguous=True)
        ACT(act_fn(smallp[:, 16:24], smallp[:, 0:8], AF.Exp), [smallp_t], [smallp_t])
        DVE(ts_fn(smallp[:, 16:24], smallp[:, 16:24], -1.0, None, ALU.mult), [smallp_t], [smallp_t])
        ws_nat = tmpA[:, 0:1024].rearrange("p (g s) -> p g s", g=8)
        LD(ws_nat, w_spatial[l].rearrange("g t s -> t g s"), [], [tmpA_t])
        for g in range(8):
            DVE(tt_fn(dlb2[:, 0, g, :], ws_nat[:, g, :], TRIL_f, ALU.mult), [tmpA_t, cst_t], [dlb2_t[0][g]])
        for hf in range(2):
            bk, bt = bank()
            for g4 in range(4):
                g = hf * 4 + g4
                MM(bk[:, g4 * 128:(g4 + 1) * 128], dlb2[:, 0, g, :], I_b, True, True, [dlb2_t[0][g], cst_t], [bt])
            DVE(cp_fn(wsT[:, hf * 4:(hf + 1) * 4, :], bk[:].rearrange("p (g t) -> p g t", g=4)), [bt], [wsT_t])
        DVE(lambda e: e.memset(S[:], 0.0), [], S_t)
        DVE(lambda e: e.memset(S_bf[:], 0.0), [], Sb_t)
        DVE(lambda e: e.memset(tails[:], 0.0), [], [tails_t])
        LD(conv_s[l, :, 0:2, :], sc[l, :, 1:3, :], [], [])

        for blk in range(nblk):
            has_s = (blk == nblk - 1)
            tiles = [(blk * 4 + i, 128, i * 128, i) for i in range(4)]
            if has_s:
                tiles.append((16, NS, TB, 4))
            ntok = TOK if has_s else TB
            LD(gslot[:], pre_norm[l].partition_broadcast(128), [], [gslot_t])
            for (ti, rows, c0, ii) in tiles:
                xt = x_tm[0:rows, ti, :]
                ACT(act_fn(jh[0:rows, :], xt, AF.Square, accum_out=st0[0:rows, 0:1]), [x_t[ti]], [jh_t, st0_t])
                rsqrt_col(st0[0:rows, 1:2], st0[0:rows, 0:1], 1, rows, 1.0 / D, [st0_t], [st0_t])
                DVE(stt_fn(jh[0:rows, :], xt, st0[0:rows, 1:2], gslot[0:rows, :], ALU.mult, ALU.mult),
                    [x_t[ti], st0_t, gslot_t], [jh_t])
                for hf in range(2):
                    bk, bt = bank()
                    for k4 in range(4):
                        kc = hf * 4 + k4
                        MM(bk[:, k4 * 128:k4 * 128 + rows], jh[0:rows, kc * 128:(kc + 1) * 128],
                           I_b[0:rows, 0:rows], True, True, [jh_t, cst_t], [bt])
                    src = bk[:].rearrange("p (k t) -> p k t", k=4)[:, :, 0:rows]
                    eng = ACT if hf == 0 else DVE
                    eng(cp_fn(hT[:, hf * 4:(hf + 1) * 4, c0:c0 + rows], src) if hf else
                        act_fn(hT[:, hf * 4:(hf + 1) * 4, c0:c0 + rows], src, AF.Copy), [bt], [hT_t[ii]])
            ck(1)

            LD(gslot[:], gmlp_norm[l].partition_broadcast(128), [], [gslot_t])
            wv = [load_w([(win_v, C_V + hf * 512, 512, 0)]) for hf in range(2)]
            for (ti, rows, c0, ii) in tiles:
                bks = []
                for hf in range(2):
                    bk, bt = bank()
                    wa, wt_ = wv[hf]
                    for kc in range(KC):
                        MM(bk[0:rows, :], hT[:, kc, c0:c0 + rows], wa[:, kc, :], kc == 0, kc == KC - 1,
                           [hT_t[ii]] + wt_, [bt])
                    ACT(act_fn(jh[0:rows, 0:512], bk[0:rows, :], AF.Square, accum_out=st0[0:rows, 2 + hf:3 + hf]),
                        [bt], [jh_t, st0_t])
                    bks.append((bk, bt))
                DVE(tt_fn(st0[0:rows, 4:5], st0[0:rows, 2:3], st0[0:rows, 3:4], ALU.add), [st0_t], [st0_t])
                rsqrt_col(st0[0:rows, 5:6], st0[0:rows, 4:5], 1, rows, 1.0 / D, [st0_t], [st0_t])
                for hf in range(2):
                    bk, bt = bks[hf]
                    if ii < 4:
                        DVE(stt_fn(vn[:, ii, hf * 512:(hf + 1) * 512], bk[:, :], st0[:, 5:6],
                                   gmlp_b[:, hf * 512:(hf + 1) * 512], ALU.mult, ALU.mult),
                            [bt, st0_t, gmlp_t], [vn_t[ii]])
                    else:
                        vs32 = tmpA[0:NS, 0:1024]
                        DVE(stt_fn(vs32[:, hf * 512:(hf + 1) * 512], bk[0:NS, :], st0[0:NS, 5:6],
                                   gmlp_b[0:NS, hf * 512:(hf + 1) * 512], ALU.mult, ALU.mult),
                            [bt, st0_t, gmlp_t], [tmpA_t])
                if ii == 4:
                    vs32 = tmpA[0:NS, 0:1024]
                    P.dma('sp', dma_fn(vrow_s[l, :, :], vs32), [tmpA_t], [])
                    DVE(cp_fn(vn_s[0:NS, :], vs32), [tmpA_t], [vns_t])
                    bk, bt = bank()
                    for g in range(8):
                        MM(bk[:, g * NS:(g + 1) * NS], vn_s[0:NS, g * 128:(g + 1) * 128], I_b[0:NS, 0:NS],
                           True, True, [vns_t, cst_t], [bt])
                    for g in range(8):
                        DVE(ts_fn(stm[:, 3, g * NS:(g + 1) * NS], bk[:, g * NS:(g + 1) * NS],
                                  smallp[:, 24 + g:25 + g], smallp[:, 32 + g:33 + g], ALU.mult, ALU.add),
                            [bt, smallp_t], [stm_t[3]])

            ck(2)
            for gh in range(2):
                wu = load_w([(win_v, C_U + gh * 512, 512, 0)])
                wz = load_w([(win_v, C_Z + gh * 512, 512, 0)])
                for g4 in range(4):
                    g = gh * 4 + g4
                    LD(bsp_b[:, g % 2, :], b_spatial[l, g].partition_broadcast(128), [], [bsp2_t[g % 2]])
                    bu, but = bank()
                    bz, bzt = bank()
                    bm, bmt = bank()
                    for kc in range(KC):
                        MM(bu[:, :], wu[0][:, kc, g4 * 128:(g4 + 1) * 128], hT[:, kc, 0:TB], kc == 0, kc == KC - 1,
                           hT_t[0:4] + wu[1], [but])
                    for kc in range(KC):
                        MM(bz[:, :], wz[0][:, kc, g4 * 128:(g4 + 1) * 128], hT[:, kc, 0:TB], kc == 0, kc == KC - 1,
                           hT_t[0:4] + wz[1], [bzt])
                    for i in range(4):
                        MM(bm[:, i * 128:(i + 1) * 128], vn[:, i, g * 128:(g + 1) * 128], wsT[:, g, :], True, True,
                           [vn_t[i], wsT_t], [bmt])
                    t_sz = tmpA[:, 0:TB]
                    t_1 = tmpA[:, TOK:TOK + TB]
                    ACT(act_fn(t_sz, bz[:, :], AF.Silu), [bzt], [tmpA_t])
                    DVE(tt_fn(t_1.rearrange("p (i t) -> p i t", i=4), bm[:].rearrange("p (i t) -> p i t", i=4),
                              bsp_b[:, g % 2, :].unsqueeze(1).to_broadcast([128, 4, 128]), ALU.add),
                        [bmt, bsp2_t[g % 2]], [tmpA_t])
                    DVE(tt_fn(t_1, t_1, t_sz, ALU.mult), [tmpA_t], [tmpA_t])
                    DVE(tt_fn(yaT[:, g, 0:TB], t_1, bu[:, :], ALU.mult), [tmpA_t, but], [ya_t[g]])
                    if has_s:
                        bs, bst = bank()
                        for kc in range(KC):
                            MM(bs[:, 0:NS], wu[0][:, kc, g4 * 128:(g4 + 1) * 128], hT[:, kc, TB:TOK], kc == 0,
                               kc == KC - 1, [hT_t[4]] + wu[1], [bst])
                        for kc in range(KC):
                            MM(bs[:, NS:2 * NS], wz[0][:, kc, g4 * 128:(g4 + 1) * 128], hT[:, kc, TB:TOK], kc == 0,
                               kc == KC - 1, [hT_t[4]] + wz[1], [bst])
                        ACT(act_fn(stt[:, 0, :], bs[:, NS:2 * NS], AF.Silu), [bst], [stt_t[0]])
                        DVE(tt_fn(stt[:, 0, :], stt[:, 0, :], stm[:, 3, g * NS:(g + 1) * NS], ALU.mult),
                            [stt_t[0], stm_t[3]], [stt_t[0]])
                        DVE(tt_fn(yaT[:, g, TB:TOK], stt[:, 0, :], bs[:, 0:NS], ALU.mult), [stt_t[0], bst], [ya_t[8]])

            ck(3)
            wab = load_w([(win_v, C_A, 16, 0)])
            for (ti, rows, c0, ii) in tiles:
                ck(3.01 + 0.01 * ii)
                bk, bt = bank()
                for kc in range(KC):
                    MM(bk[0:rows, 0:16], hT[:, kc, c0:c0 + rows], wab[0][:, kc, 0:16], kc == 0, kc == KC - 1,
                       [hT_t[ii]] + wab[1], [bt])
                ACT(act_fn(scal[0:rows, ii, 0:8], bk[0:rows, 8:16], AF.Sigmoid), [bt], [scal_t[ii]])
                DVE(tt_fn(scal[0:rows, 5 + ii, 0:8], bk[0:rows, 0:8], smallp[0:rows, 8:16], ALU.add),
                    [bt, smallp_t, scal_t[ii]], [scal_t[5 + ii]])
                ACT(act_fn(scal[0:rows, 5 + ii, 0:8], scal[0:rows, 5 + ii, 0:8], AF.Exp), [scal_t[5 + ii]],
                    [scal_t[5 + ii]])
                ACT(act_fn(scal[0:rows, 5 + ii, 0:8], scal[0:rows, 5 + ii, 0:8], AF.Ln, bias=ONES_f[0:rows, 0:1], scale=1.0),
                    [scal_t[5 + ii], cst_t], [scal_t[5 + ii]])
                DVE(tt_fn(scal[0:rows, 5 + ii, 0:8], scal[0:rows, 5 + ii, 0:8], smallp[0:rows, 16:24], ALU.mult),
                    [scal_t[5 + ii], smallp_t], [scal_t[5 + ii]])
                ck(3.015 + 0.01 * ii)
                if ii < 4:
                    bk2, bt2 = bank()
                    MM(bk2[:, 0:8], U_f, scal[:, 5 + ii, 0:8], True, True, [cst_t, scal_t[5 + ii]], [bt2])
                    MM(bk2[:, 8:16], ONES_f, scal[:, 5 + ii, 0:8], True, True, [cst_t, scal_t[5 + ii]], [bt2])
                    DVE(cp_fn(scal[:, 10 + ii, 0:16], bk2[:, 0:16]), [bt2], [scal_t[10 + ii]])
                    gc8, gl8 = scal[:, 10 + ii, 0:8], scal[:, 10 + ii, 8:16]
                    sA, sB = [scal_t[10 + ii]], [sc2_t[ii]]
                    ACT(act_fn(sc2[:, ii, 0, :], gc8, AF.Exp), sA, sB)
                    DVE(tt_fn(sc2[:, ii, 1, :], gl8, gc8, ALU.subtract), sA, sB)
                    ACT(act_fn(sc2[:, ii, 1, :], sc2[:, ii, 1, :], AF.Exp), sB, sB)
                    ACT(act_fn(sc2[:, ii, 2, :], gl8, AF.Exp), sA, sB)
                    DVE(ts_fn(sc2[:, ii, 3, :], sc2[:, ii, 0, :], -1.0, None, ALU.mult), sB, sB)
                    DVE(ts_fn(sc2[:, ii, 4, :], gc8, -1.0, None, ALU.mult), sA, sB)
                else:
                    ACT(act_fn(scal[0:NS, 5 + ii, 0:8], scal[0:NS, 5 + ii, 0:8], AF.Exp), [scal_t[5 + ii]], [scal_t[5 + ii]])
                    for which, srcsl in ((0, scal[0:NS, ii, 0:8]), (1, scal[0:NS, 5 + ii, 0:8])):
                        DVE(lambda e, which=which: e.memset(stm[:, which, :], 0.0), [], [stm_t[which]])
                        for s2 in range(NS):
                            DVE(ts_fn(stm[0:NS, which, s2 * 8:(s2 + 1) * 8], srcsl, I_f[0:NS, s2:s2 + 1], None, ALU.mult),
                                [scal_t[ii], scal_t[5 + ii], cst_t], [stm_t[which]])
                    ck(3.07)
                    bk2, bt2 = bank()
                    MM(bk2[:, 0:128], ONES_f, stm[:, 0, :], True, True, [cst_t, stm_t[0]], [bt2])
                    MM(bk2[:, 128:256], ONES_f, stm[:, 1, :], True, True, [cst_t, stm_t[1]], [bt2])
                    ck(3.08)
                    DVE(cp_fn(stm[:, 0, :], bk2[:, 0:128]), [bt2], [stm_t[0]])
                    DVE(cp_fn(stm[:, 1, :], bk2[:, 128:256]), [bt2], [stm_t[1]])

            ck(3.1)
            def front(h, bs, bs2=None):
                qn, kn, vT, szbT = qn2[:, bs, :], kn2[:, bs, :], vT2[:, bs, :], szb2[:, bs, :]
                qn_t, kn_t, vT_t, szb_t = qn2_t[bs], kn2_t[bs], vT2_t[bs], szb2_t[bs]
                wh = load_w([(win_v, C_Q + h * 128, 128, 0), (win_v, C_K + h * 128, 128, 128),
                             (win_v, C_VV + h * 128, 128, 256), (win_v, C_ZB + h * 128, 128, 384)])
                if has_s:
                    bk, bt = bank()
                    for kc in range(KC):
                        MM(bk[0:NS, 0:384], hT[:, kc, TB:TOK], wh[0][:, kc, 0:384], kc == 0, kc == KC - 1,
                           [hT_t[4]] + wh[1], [bt])
                    ACT(act_fn(cstage[0:NS, :, :].rearrange("p a b -> p (a b)"), bk[0:NS, 0:384], AF.Copy), [bt], [cstage_t])
                    for j in range(3):
                        c0_ = j * 1024 + h * 128
                        P.dma('sp', dma_fn(conv_s[l, :, 2, c0_:c0_ + 128], cstage[0:NS, j, :]), [cstage_t], [])
                for j, (dst, dtok) in enumerate(((qn, qn_t), (kn, kn_t), (vT, vT_t))):
                    ch = j * 8 + h
                    pb = 0
                    bk, bt = bank()
                    for kc in range(KC):
                        MM(bk[:, :], wh[0][:, kc, j * 128:(j + 1) * 128], hT[:, kc, 0:TB], kc == 0, kc == KC - 1,
                           hT_t[0:4] + wh[1], [bt])
                    pj = pre[:, pb, :]
                    ACT(act_fn(pj[:, 3:3 + TB], bk[:, :], AF.Copy), [bt], [pre_t[pb]])
                    DVE(cp_fn(pj[:, 0:3], tails[:, ch, :]), [tails_t], [pre_t[pb]])
                    DVE(cp_fn(tails[:, ch, :], pj[:, TB:TB + 3]), [pre_t[pb]], [tails_t])
                    if blk == nblk - 1:
                        P.dma('sp', dma_fn(conv_p[l, :, ch * 128:(ch + 1) * 128].rearrange("i c -> c i"),
                                           tails[:, ch, :], allow_slow_non_contiguous=True), [tails_t], [])
                    cw = convw[:, 0, ch, :]
                    DVE(ts_fn(acc[:, 0:TB], pj[:, 0:TB], cw[:, 0:1], None, ALU.mult), [pre_t[pb], convw_t], [acc_t])
                    for i in range(1, 4):
                        DVE(stt_fn(acc[:, 0:TB], pj[:, i:i + TB], cw[:, i:i + 1], acc[:, 0:TB], ALU.mult, ALU.add),
                            [pre_t[pb], convw_t, acc_t], [acc_t])
                    if has_s:
                        P.dma('sp', dma_fn(cstage[0:NS, :, :], sc[l, :, :, ch * 128:(ch + 1) * 128]), [], [cstage_t])
                        bs, bst = bank()
                        for i in range(3):
                            MM(bs[:, i * NS:(i + 1) * NS], cstage[:, i, :], I_f[:, 0:NS], True, True,
                               [cstage_t, cst_t], [bst])
                        for kc in range(KC):
                            MM(bs[:, 3 * NS:4 * NS], wh[0][:, kc, j * 128:(j + 1) * 128], hT[:, kc, TB:TOK],
                               kc == 0, kc == KC - 1, [hT_t[4]] + wh[1], [bst])
                        DVE(ts_fn(acc[:, TB:TOK], bs[:, 0:NS], cw[:, 0:1], None, ALU.mult), [bst, convw_t], [acc_t])
                        for i in range(1, 4):
                            DVE(stt_fn(acc[:, TB:TOK], bs[:, i * NS:(i + 1) * NS], cw[:, i:i + 1], acc[:, TB:TOK],
                                       ALU.mult, ALU.add), [bst, convw_t, acc_t], [acc_t])
                    if j < 2:
                        ACT(act_fn(acc[:, 0:ntok], acc[:, 0:ntok], AF.Silu), [acc_t], [acc_t])
                        ACT(act_fn(sq[:, 0:ntok], acc[:, 0:ntok], AF.Square), [acc_t], [sq_t])
                        bk2, bt2 = bank()
                        MM(bk2[:, 0:TB], ONES_b, sq[:, 0:TB], True, True, [cst_t, sq_t], [bt2])
                        rsqrt_col(rb[:, 0:TB], bk2[:, 0:TB], TB, 128, 1.0, [bt2], [rb_t])
                        if has_s:
                            bk3, bt3 = bank()
                            MM(bk3[:, 0:NS], ONES_b, sq[:, TB:TOK], True, True, [cst_t, sq_t], [bt3])
                            rsqrt_col(rb[:, TB:TOK], bk3[:, 0:NS], NS, 128, 1.0, [bt3], [rb_t])
                        sc_ = (128.0 ** -0.5) if j == 0 else 1.0
                        DVE(stt_fn(dst[:, 0:ntok], acc[:, 0:ntok], sc_, rb[:, 0:ntok], ALU.mult, ALU.mult),
                            [acc_t, rb_t], [dtok])
                    else:
                        ACT(act_fn(dst[:, 0:ntok], acc[:, 0:ntok], AF.Silu), [acc_t], [dtok])
                bk, bt = bank()
                for kc in range(KC):
                    MM(bk[:, :], wh[0][:, kc, 384:512], hT[:, kc, 0:TB], kc == 0, kc == KC - 1,
                       hT_t[0:4] + wh[1], [bt])
                ACT(act_fn(szbT[:, 0:TB], bk[:, :], AF.Silu), [bt], [szb_t])
                if has_s:
                    bs, bst = bank()
                    for kc in range(KC):
                        MM(bs[:, 0:NS], wh[0][:, kc, 384:512], hT[:, kc, TB:TOK], kc == 0,
                           kc == KC - 1, [hT_t[4]] + wh[1], [bst])
                    ACT(act_fn(szbT[:, TB:TOK], bs[:, 0:NS], AF.Silu), [bst], [szb_t])

                ck(3.2)
            def deltaA(h, bs3, bs):
                qn, kn, vT, szbT = qn2[:, bs3, :], kn2[:, bs3, :], vT2[:, bs3, :], szb2[:, bs3, :]
                qn_t, kn_t, vT_t, szb_t = qn2_t[bs3], kn2_t[bs3], vT2_t[bs3], szb2_t[bs3]
                c_ = lambda i: slice(i * 128, (i + 1) * 128)
                Qf = lambda i: nmr[:, 0, i, :]
                Lf = lambda i: nmr[:, 1, i, :]
                Mf = lambda i: nmr[:, 2, i, :]
                DT = lambda i: nm[:, 0, i, :]
                VT = lambda i: vt2[:, bs, i, :]
                fl = lambda k: (nm[:, 0, :, :] if k == 3 else nmr[:, k, :, :]).rearrange("p t c -> p (t c)")
                At = lambda i: dlb2[:, bs, i, :]
                kd = lambda i: dlb2[:, bs, 4 + i, :]
                Wt = lambda i: dlb2[:, bs, 8 + i, :]
                dlb_t = dlb2_t[bs]
                nQ, nL, nM, nD, nV = nm_t[0:4], nm_t[4:8], nm_t[8:12], nm_t[12:16], vt2_t[bs]
                bg, bgt = bank()
                for i in range(4):
                    DVE(ts_fn(DT(i), U_f, scal[:, 5 + i, h:h + 1], None, ALU.mult), [cst_t, scal_t[5 + i]], [nD[i]])
                    MM(bg[:, c_(i)], ONES_f, DT(i), True, True, [cst_t, nD[i]], [bgt])
                for i in range(4):
                    DVE(stt_fn(DT(i), bg[:, c_(i)], sc2[:, i, 4, h:h + 1], NM_f, ALU.add, ALU.add),
                        [bgt, cst_t, sc2_t[i]], [nD[i]])
                ACT(act_fn(fl(3), fl(3), AF.Exp), nD, nD)
                bkk, bkkt = bank()
                bat, batt = bank()
                for i in range(4):
                    MM(bkk[:, c_(i)], kn[:, c_(i)], kn[:, c_(i)], True, True, [kn_t], [bkkt])
                for i in range(4):
                    MM(bat[:, c_(i)], kn[:, c_(i)], qn[:, c_(i)], True, True, [kn_t, qn_t], [batt])
                for i in range(4):
                    DVE(stt_fn(Qf(i), bkk[:, c_(i)], scal[:, i, h:h + 1], DT(i), ALU.mult, ALU.mult),
                        [bkkt, nD[i], scal_t[i]], [nQ[i]])
                DVE(tt_fn(dlb2[:, bs, 0:4, :].rearrange("p t c -> p (t c)"), bat[:, :], fl(3), ALU.mult), [batt] + nD,
                    dlb_t[0:4])
                DVE(tt_fn(nmr[:, 0, :, :], nmr[:, 0, :, :], SU_f.unsqueeze(1).to_broadcast([128, 4, 128]), ALU.mult),
                    nQ + [cst_t], nQ)
                DVE(stt_fn(nmr[:, 2, :, :], nmr[:, 0, :, :], -1.0, I_f.unsqueeze(1).to_broadcast([128, 4, 128]),
                           ALU.mult, ALU.add), nQ + [cst_t], nM)
                bl, blt = bank()
                for i in range(4):
                    MM(bl[:, c_(i)], Qf(i), idr[:], True, True, [nQ[i], cst_t], [blt])
                ACT(act_fn(fl(1), bl[:, :], AF.Copy), [blt], nL)
                for lev in range(1, 7):
                    b1, b1t = bank()
                    for i in range(4):
                        MM(b1[:, c_(i)], Qf(i), Lf(i), True, True, [nQ[i], nL[i]], [b1t])
                    if lev < 6:
                        b3, b3t = bank()
                        for i in range(4):
                            MM(b3[:, c_(i)], Lf(i), Qf(i), True, True, [nQ[i], nL[i]], [b3t])
                    ACT(act_fn(fl(1), b1[:, :], AF.Copy), [b1t], nL)
                    if lev < 6:
                        DVE(cp_fn(fl(0), b3[:, :]), [b3t], nQ)
                    b2, b2t = bank()
                    for i in range(4):
                        MM(b2[:, c_(i)], Lf(i), Mf(i), True, True, [nL[i], nM[i]], [b2t])
                    if lev < 6:
                        DVE(tt_fn(fl(2), b2[:, :], fl(2), ALU.add), [b2t] + nM, nM)
                    else:
                        DVE(tt_fn(dlb2[:, bs, 8:12, :].rearrange("p t c -> p (t c)"), b2[:, :], fl(2), ALU.add),
                            [b2t] + nM, dlb_t[8:12])
                btk, btkt = bank()
                btv, btvt = bank()
                for i in range(4):
                    MM(btk[:, c_(i)], kn[:, c_(i)], I_b, True, True, [kn_t, cst_t], [btkt])
                for i in range(4):
                    MM(btv[:, c_(i)], vT[:, c_(i)], I_b, True, True, [vT_t, cst_t], [btvt])
                for i in range(4):
                    ACT(act_fn(kd(i), btk[:, c_(i)], AF.Copy, scale=sc2[:, i, 1, h:h + 1]), [btkt, sc2_t[i]],
                        [dlb_t[4 + i]])
                ACT(act_fn(vt2[:, bs, :, :].rearrange("p t c -> p (t c)"), btv[:, :], AF.Copy), [btvt], nV)
            def deltaB(h, bs3, bs):
                qn, kn, vT, szbT = qn2[:, bs3, :], kn2[:, bs3, :], vT2[:, bs3, :], szb2[:, bs3, :]
                qn_t, kn_t, vT_t, szb_t = qn2_t[bs3], kn2_t[bs3], vT2_t[bs3], szb2_t[bs3]
                c_ = lambda i: slice(i * 128, (i + 1) * 128)
                VT = lambda i: vt2[:, bs, i, :]
                At = lambda i: dlb2[:, bs, i, :]
                kd = lambda i: dlb2[:, bs, 4 + i, :]
                Wt = lambda i: dlb2[:, bs, 8 + i, :]
                dlb_t = dlb2_t[bs]
                nV = vt2_t[bs]
                for i in range(4):
                    cs = c_(i)
                    beta = scal[:, i, h:h + 1]
                    eg, egl, negeg = sc2[:, i, 0, h:h + 1], sc2[:, i, 2, h:h + 1], sc2[:, i, 3, h:h + 1]
                    bp, bpt = bank()
                    MM(bp[:, 0:128], kn[:, cs], S_bf[:, h, :], True, True, [kn_t, Sb_t[h]], [bpt])
                    bp2, bp2t = bank()
                    MM(bp2[:, 0:128], qn[:, cs], S_bf[:, h, :], True, True, [qn_t, Sb_t[h]], [bp2t])
                    Z, Zt = dlc[:, 0, :], dlc_t[0]
                    DVE(stt_fn(Z, bp[:, 0:128], negeg, VT(i), ALU.mult, ALU.add), [bpt, nV[i], sc2_t[i]], [Zt])
                    bv, bvt = bank()
                    MM(bv[:, 0:128], Wt(i), Z, True, True, [dlb_t[8 + i], Zt], [bvt])
                    vnw, vnwt = dlc[:, 1, :], dlc_t[1]
                    ACT(act_fn(vnw, bv[:, 0:128], AF.Copy, scale=beta), [bvt, scal_t[i]], [vnwt])
                    ACT(act_fn(dl[:, 0, :], bp2[:, 0:128], AF.Copy, scale=eg), [bp2t, sc2_t[i]], [dl_t[0]])
                    bo, bot = bank()
                    MM(bo[:, 0:128], At(i), vnw, True, True, [dlb_t[i], vnwt], [bot])
                    MM(bo[:, 128:256], kd(i), vnw, True, True, [dlb_t[4 + i], vnwt], [bot])
                    DVE(stt_fn(S_bf[:, h, :], S[:, h, :], egl, bo[:, 128:256], ALU.mult, ALU.add),
                        [S_t[h], bot, sc2_t[i]], [Sb_t[h]])
                    DVE(stt_fn(S[:, h, :], S[:, h, :], egl, bo[:, 128:256], ALU.mult, ALU.add),
                        [S_t[h], bot, sc2_t[i]], [S_t[h]])
                    DVE(tt_fn(dl[:, 0, :], dl[:, 0, :], bo[:, 0:128], ALU.add), [dl_t[0], bot], [dl_t[0]])
                    ACT(act_fn(dl[:, 1, :], dl[:, 0, :], AF.Square, accum_out=st0[:, 2:3]), [dl_t[0]], [dl_t[1], st0_t])
                    rsqrt_col(st0[:, 3:4], st0[:, 2:3], 1, 128, 1.0 / 128, [st0_t], [st0_t])
                    yb, ybt_ = dlc[:, 2, :], dlc_t[2]
                    DVE(stt_fn(yb, dl[:, 0, :], st0[:, 3:4], gdn_b[:, :], ALU.mult, ALU.mult),
                        [dl_t[0], st0_t, gdn_t], [ybt_])
                    by, byt = bank()
                    MM(by[:, 0:128], yb, I_b, True, True, [ybt_, cst_t], [byt])
                    DVE(tt_fn(ybT[:, h, cs], by[:, 0:128], szbT[:, cs], ALU.mult), [byt, szb_t], [yb_t[h]])
                ck(3.3)
                if blk == nblk - 1:
                    P.dma('sp', dma_fn(gdn_p[l, h, :, :], S[:, h, :]), [S_t[h]], [])

            def deltaS(h, bs3):
                qn, kn, vT, szbT = qn2[:, bs3, :], kn2[:, bs3, :], vT2[:, bs3, :], szb2[:, bs3, :]
                qn_t, kn_t, vT_t, szb_t = qn2_t[bs3], kn2_t[bs3], vT2_t[bs3], szb2_t[bs3]
                if has_s:
                    ss = slice(TB, TOK)
                    beta_b = stm[:, 0, :].rearrange("p (s h) -> p s h", s=NS)[:, :, h]
                    eg_b = stm[:, 1, :].rearrange("p (s h) -> p s h", s=NS)[:, :, h]
                    kq = dl[:, 2, 0:2 * NS].rearrange("p (s two) -> p s two", two=2)
                    DVE(cp_fn(kq[:, :, 0], kn[:, ss]), [kn_t], [dl_t[2]])
                    DVE(cp_fn(kq[:, :, 1], qn[:, ss]), [qn_t], [dl_t[2]])
                    bst_, bstt = bank()
                    for s in range(NS):
                        sl = s % 4
                        P.dma('sp', dma_fn(sst_in[:, sl, :], sg[l, s, h, :, :]), [], [ssi_t[sl]])
                        MM(bst_[:, 2 * s:2 * s + 2], sst_in[:, sl, :], kq[:, s, :], True, True, [ssi_t[sl], dl_t[2]],
                           [bstt])
                    stkq = bst_[:, 0:2 * NS].rearrange("p (s two) -> p s two", two=2)
                    DVE(tt_fn(stt[:, 2, :], stkq[:, :, 0], eg_b, ALU.mult), [bstt, stm_t[1]], [stt_t[2]])
                    DVE(tt_fn(stt[:, 2, :], vT[:, ss], stt[:, 2, :], ALU.subtract), [vT_t, stt_t[2]], [stt_t[2]])
                    DVE(tt_fn(stt[:, 2, :], stt[:, 2, :], beta_b, ALU.mult), [stt_t[2], stm_t[0]], [stt_t[2]])
                    DVE(tt_fn(stt[:, 3, :], kq[:, :, 0], kq[:, :, 1], ALU.mult), [dl_t[2]], [stt_t[3]])
                    bqk, bqkt = bank()
                    MM(bqk[:, 0:NS], ONES_f, stt[:, 3, :], True, True, [cst_t, stt_t[3]], [bqkt])
                    DVE(tt_fn(stt[:, 4, :], stkq[:, :, 1], eg_b, ALU.mult), [bstt, stm_t[1]], [stt_t[4]])
                    DVE(tt_fn(stt[:, 3, :], bqk[:, 0:NS], stt[:, 2, :], ALU.mult), [bqkt, stt_t[2]], [stt_t[3]])
                    DVE(tt_fn(stt[:, 4, :], stt[:, 4, :], stt[:, 3, :], ALU.add), [stt_t[3], stt_t[4]], [stt_t[4]])
                    btp, btpt = bank()
                    MM(btp[0:NS, 0:128], stt[:, 2, :], I_f, True, True, [stt_t[2], cst_t], [btpt])
                    MM(btp[0:NS, 128:256], kq[:, :, 0], I_f, True, True, [dl_t[2], cst_t], [btpt])
                    MM(btp[0:NS, 256:384], stt[:, 4, :], I_f, True, True, [stt_t[4], cst_t], [btpt])
                    DVE(cp_fn(stm[0:NS, 2, :], btp[0:NS, 0:128]), [btpt], [stm_t[2]])
                    DVE(cp_fn(spad[0:NS, 0, :], btp[0:NS, 128:256]), [btpt], [spad_t[0]])
                    ACT(act_fn(stmb[0:NS, 1, :], btp[0:NS, 256:384], AF.Square, accum_out=st0[0:NS, 6:7]),
                        [btpt, stm_t[2], spad_t[0]], [stmb_t[1], st0s_t])
                    rsqrt_col(st0[0:NS, 7:8], st0[0:NS, 6:7], 1, NS, 1.0 / 128, [st0s_t], [st0s_t])
                    DVE(stt_fn(stmb[0:NS, 0, :], btp[0:NS, 256:384], st0[0:NS, 7:8], gdn_b[0:NS, :], ALU.mult, ALU.mult),
                        [btpt, st0s_t, gdn_t], [stmb_t[0]])
                    by, byt = bank()
                    MM(by[:, 0:NS], stmb[0:NS, 0, :], I_b[0:NS, 0:NS], True, True, [stmb_t[0], cst_t], [byt])
                    DVE(tt_fn(ybT[:, h, ss], by[:, 0:NS], szbT[:, ss], ALU.mult), [byt, szb_t], [yb_t[8]])
                    if h == 0:
                        DVE(lambda e: e.memset(stm[:, 3, :], 0.0), [], [stm_t[3]])
                    msk = [(spad[:, 1, :], spad[0:NS, 1, :], spad_t[1]), (stm[:, 3, :], stm[0:NS, 3, :], stm_t[3])]
                    for s in range(4):
                        P.dma('sp', dma_fn(sst_in[:, s % 4, :], sg[l, s, h, :, :]), [], [ssi_t[s % 4]])
                    for s0 in range(0, NS, 2):
                        for u in range(2):
                            s = s0 + u
                            DVE(ts_fn(msk[u][1], stm[0:NS, 2, :], I_f[0:NS, s:s + 1], None, ALU.mult),
                                [stm_t[2], cst_t], [msk[u][2]])
                        bop, bopt = bank()
                        for u in range(2):
                            MM(bop[:, u * 128:(u + 1) * 128], spad[:, 0, :], msk[u][0], True, True,
                               [spad_t[0], msk[u][2]], [bopt])
                        for u in range(2):
                            s = s0 + u
                            sl = s % 4
                            DVE(stt_fn(sst_out[:, u, :], sst_in[:, sl, :], eg_b[:, s:s + 1], bop[:, u * 128:(u + 1) * 128],
                                       ALU.mult, ALU.add), [ssi_t[sl], stm_t[1], bopt], [sso_t[u]])
                            P.dma('sp', dma_fn(gdn_s[l, s, h, :, :], sst_out[:, u, :]), [sso_t[u]], [])
                        for u in range(2):
                            s = s0 + 4 + u
                            if s < NS:
                                P.dma('sp', dma_fn(sst_in[:, s % 4, :], sg[l, s, h, :, :]), [], [ssi_t[s % 4]])

            BF_, BA_, BB_, BS_ = [0, 1], [2, 3], [4, 5], [6, 7]
            QF_, QA_, QB_, QS_ = (4, 4, 3, 3) if has_s else (8, 8, 4, 3)
            front(0, 0)
            co_run(P, [((lambda: deltaA(0, 0, 0)), QA_, BA_), ((lambda: front(1, 1)), QF_, BF_)])
            for h in range(H):
                tasks = [((lambda h=h: deltaB(h, h % 3, h % 2)), QB_, BB_)]
                if has_s:
                    tasks.append(((lambda h=h: deltaS(h, h % 3)), QS_, BS_))
                if h + 1 < H:
                    tasks.append(((lambda h=h: deltaA(h + 1, (h + 1) % 3, (h + 1) % 2)), QA_, BA_))
                if h + 2 < H:
                    tasks.append(((lambda h=h: front(h + 2, (h + 2) % 3)), QF_, BF_))
                co_run(P, tasks)

            ck(4)
            ybr = yb_t[0:9] if has_s else yb_t[0:8]
            yar = ya_t[0:9] if has_s else ya_t[0:8]
            for qd in range(4):
                wa = load_w([(wpa_v, qd * 256, 256, 0), (win_v, C_GA + qd * 256, 256, 256)])
                wb = load_w([(wpb_v, qd * 256, 256, 0), (win_v, C_GB + qd * 256, 256, 256)])
                for j2 in range(2):
                    j = qd * 2 + j2
                    groups = [(0, TB, hT_t[0:4])] + ([(TB, NS, [hT_t[4]])] if has_s else [])
                    for (c0, n, htk) in groups:
                        bpa, bpat = bank()
                        bpb, bpbt = bank()
                        bga, bgat = bank()
                        bgb, bgbt = bank()
                        for kc in range(KC):
                            MM(bpa[:, 0:n], wa[0][:, kc, j2 * 128:(j2 + 1) * 128], yaT[:, kc, c0:c0 + n], kc == 0,
                               kc == KC - 1, yar + wa[1], [bpat])
                        for kc in range(KC):
                            MM(bpb[:, 0:n], wb[0][:, kc, j2 * 128:(j2 + 1) * 128], ybT[:, kc, c0:c0 + n], kc == 0,
                               kc == KC - 1, ybr + wb[1], [bpbt])
                        for kc in range(KC):
                            MM(bga[:, 0:n], wa[0][:, kc, 256 + j2 * 128:256 + (j2 + 1) * 128], hT[:, kc, c0:c0 + n],
                               kc == 0, kc == KC - 1, htk + wa[1], [bgat])
                        for kc in range(KC):
                            MM(bgb[:, 0:n], wb[0][:, kc, 256 + j2 * 128:256 + (j2 + 1) * 128], hT[:, kc, c0:c0 + n],
                               kc == 0, kc == KC - 1, htk + wb[1], [bgbt])
                        ta = tmpA[:, 0:n]
                        tb_ = tmpA[:, TOK:TOK + n]
                        ACT(act_fn(ta, bga[:, 0:n], AF.Sigmoid), [bgat], [tmpA_t])
                        ACT(act_fn(tb_, bgb[:, 0:n], AF.Sigmoid), [bgbt], [tmpA_t])
                        DVE(tt_fn(ta, ta, bpa[:, 0:n], ALU.mult), [tmpA_t, bpat], [tmpA_t])
                        DVE(tt_fn(tb_, tb_, bpb[:, 0:n], ALU.mult), [tmpA_t, bpbt], [tmpA_t])
                        DVE(tt_fn(mg[:, j, c0:c0 + n], ta, tb_, ALU.add), [tmpA_t], [mg_t[j]] if c0 == 0 else [mg_s_t])

            ck(5)
            LD(gslot[:], post_norm[l].partition_broadcast(128), [], [gslot_t])
            wo = [load_w([(wo_v, hf * 512, 512, 0)]) for hf in range(2)]
            for (ti, rows, c0, ii) in tiles:
                bks = []
                for hf in range(2):
                    bk, bt = bank()
                    for kc in range(KC):
                        MM(bk[0:rows, :], mg[:, kc, c0:c0 + rows], wo[hf][0][:, kc, :], kc == 0, kc == KC - 1,
                           (mg_t if ii < 4 else [mg_s_t]) + wo[hf][1], [bt])
                    ACT(act_fn(jh[0:rows, 0:512], bk[0:rows, :], AF.Square, accum_out=st0[0:rows, 2 + hf:3 + hf]),
                        [bt], [jh_t, st0_t])
                    bks.append((bk, bt))
                DVE(tt_fn(st0[0:rows, 4:5], st0[0:rows, 2:3], st0[0:rows, 3:4], ALU.add), [st0_t], [st0_t])
                rsqrt_col(st0[0:rows, 5:6], st0[0:rows, 4:5], 1, rows, 1.0 / D, [st0_t], [st0_t])
                for hf in range(2):
                    bk, bt = bks[hf]
                    t1 = tmpA[0:rows, 0:512]
                    DVE(stt_fn(t1, bk[0:rows, :], st0[0:rows, 5:6], gslot[0:rows, hf * 512:(hf + 1) * 512], ALU.mult,
                               ALU.mult), [bt, st0_t, gslot_t], [tmpA_t])
                    xs = x_tm[0:rows, ti, hf * 512:(hf + 1) * 512]
                    DVE(tt_fn(xs, xs, t1, ALU.add), [tmpA_t, x_t[ti]], [x_t[ti]])
                if l == depth - 1:
                    if ii < 4:
                        P.dma('sp', dma_fn(y_p[ti * 128:(ti + 1) * 128, :], x_tm[:, ti, :]), [x_t[ti]], [])
                    else:
                        P.dma('sp', dma_fn(y_s[:, :], x_tm[0:NS, 16, :]), [x_t[16]], [])

    sems = {}
    for e in ENGS:
        sems[e] = es.enter_context(nc.semaphore("s_" + e))
    for key in P.dma_uses:
        sems[key] = es.enter_context(nc.semaphore("d_%s_%d" % key))
    final_waits = [(k, 16 * u) for k, u in P.dma_uses.items()] + [(e, P.cnt[e]) for e in ENGS if e != 'sp' and P.cnt[e]]
    P.q['sp'].append((final_waits, None, None))

    def replay(name, e):
        for waits, fn, sig in P.q[name]:
            for k, v in waits:
                e.wait_ge(sems[k], v)
            if fn is None:
                continue
            ins = fn(e)
            if sig is None:
                continue
            if isinstance(sig, tuple):
                ins.then_inc(sems[sig], 16)
            else:
                ins.then_inc(sems[sig], 1)

    with nc.Block() as block:
        @block.tensor
        def _(e):
            replay('pe', e)

        @block.scalar
        def _(e):
            replay('act', e)

        @block.vector
        def _(e):
            replay('dve', e)

        @block.gpsimd
        def _(e):
            replay('pool', e)

        @block.sync
        def _(e):
            replay('sp', e)
    es.close()
    return nc


def make_consts():
    c = np.zeros((128, 6, 128), np.float32)
    idx = np.arange(128)
    c[:, 0, :] = np.eye(128)
    c[:, 1, :] = (idx[:, None] <= idx[None, :])
    c[:, 2, :] = (idx[None, :] > idx[:, None])
    c[:, 3, :] = np.where(idx[None, :] >= idx[:, None], 0.0, -1e30)
    c[:, 4, :] = 1.0
    c[:, 5, :] = (idx[None, :] <= idx[:, None])
    return np.ascontiguousarray(c.reshape(128, 768))


def kernel(x_prompt, x_sample, state_gdn, state_conv, pre_norm, w_in, gmlp_norm, w_spatial, b_spatial, conv_w,
           a_log, dt_bias, gdn_norm, w_proj_a, w_proj_b, w_out, post_norm):
    f = lambda a: np.ascontiguousarray(np.asarray(a, dtype=np.float32))
    nc = build()
    consts = make_consts()
    shared = dict(pre_norm=f(pre_norm), w_in=f(w_in), gmlp_norm=f(gmlp_norm), w_spatial=f(w_spatial),
                  b_spatial=f(b_spatial), conv_w=f(conv_w), a_log=f(a_log), dt_bias=f(dt_bias), gdn_norm=f(gdn_norm),
                  w_proj_a=f(w_proj_a), w_proj_b=f(w_proj_b), w_out=f(w_out), post_norm=f(post_norm), consts=consts)
    in_maps = []
    for c in range(8):
        m = dict(shared)
        m["x_p"] = f(x_prompt[c])
        m["x_s"] = f(x_sample[c * NS:(c + 1) * NS, 0, :])
        m["sg"] = f(state_gdn[:, c * NS:(c + 1) * NS])
        m["sc"] = f(state_conv[:, c * NS:(c + 1) * NS])
        in_maps.append(m)
    res = run_bass_kernel_spmd(nc, in_maps, core_ids=list(range(8)))
    r = res.results
    y_prompt = np.stack([r[c]["y_p"] for c in range(8)], axis=0)
    y_sample = np.concatenate([r[c]["y_s"] for c in range(8)], axis=0)[:, None, :]
    gdn_prompt = np.stack([r[c]["gdn_p"] for c in range(8)], axis=1)
    conv_prompt = np.stack([r[c]["conv_p"] for c in range(8)], axis=1)
    gdn_sample = np.concatenate([r[c]["gdn_s"] for c in range(8)], axis=1)
    conv_sample = np.concatenate([r[c]["conv_s"] for c in range(8)], axis=1)
    vrow = np.concatenate([r[c]["vrow_s"] for c in range(8)], axis=1)[:, :, None, :]
    return (y_prompt.astype(np.float32), y_sample.astype(np.float32), gdn_prompt.astype(np.float32),
            conv_prompt.astype(np.float32), gdn_sample.astype(np.float32), conv_sample.astype(np.float32),
            vrow.astype(np.float32))
```

```python
import threading
import numpy as np
from contextlib import ExitStack
import concourse.bass as bass
import concourse.mybir as mybir
from concourse.bass_utils import run_bass_kernel_spmd

F32 = mybir.dt.float32
BF16 = mybir.dt.bfloat16
F32R = mybir.dt.float32r
AF = mybir.ActivationFunctionType
ALU = mybir.AluOpType

D = 1024
KC = 8
TB = 512
NBLK = 4
NS = 16
L = 4
H = 8
TOK = TB + NS
IN_DIM = 9232
C_U, C_V, C_Z, C_Q, C_K, C_VV, C_ZB, C_A, C_GA, C_GB = 0, 1024, 2048, 3072, 4096, 5120, 6144, 7168, 7184, 8208
EPS = 1e-6
NSLOT = 4
NRING = 24
ENGS = ('pe', 'act', 'dve', 'pool', 'sp')
_TL = threading.local()


class Tok:
    __slots__ = ('w', 'r', 'al')

    def __init__(self):
        self.w = None
        self.r = {}
        self.al = []


def toks(n):
    return [Tok() for _ in range(n)]


class Prog:
    def __init__(self):
        self.q = {e: [] for e in ENGS}
        self.cnt = {e: 0 for e in ENGS}
        self.waited = {e: {} for e in ENGS}
        self.dma_idx = {'sp': 0, 'pool': 0}
        self.dma_uses = {}
        self.stopped = False
        self.tick = None

    def _deps(self, reads, writes):
        deps = {}

        def add(k, v):
            if deps.get(k, 0) < v:
                deps[k] = v

        for t in reads:
            for a in [t] + t.al:
                if a.w is not None:
                    add(*a.w)
        for t in writes:
            for a in [t] + t.al:
                if a.w is not None:
                    add(*a.w)
                for k, v in a.r.items():
                    add(k, v)
        return deps

    def _waits(self, eng, deps):
        waits = []
        for k, v in deps.items():
            if k == eng and eng == 'pe':
                continue
            if self.waited[eng].get(k, 0) >= v:
                continue
            self.waited[eng][k] = v
            waits.append((k, v))
        return waits

    def emit(self, eng, fn, reads=(), writes=(), signal=True):
        if self.stopped:
            return
        deps = self._deps(reads, writes)
        waits = self._waits(eng, deps)
        if signal:
            self.cnt[eng] += 1
            n = self.cnt[eng]
        else:
            n = self.cnt[eng] + 1
        self.q[eng].append((waits, fn, eng if signal else None))
        for t in reads:
            if t.r.get(eng, 0) < n:
                t.r[eng] = n
        for t in writes:
            t.w = (eng, n)
            t.r = {}
        if self.tick is not None:
            self.tick()

    def dma(self, queue, fn, reads=(), writes=()):
        if self.stopped:
            return
        slot = self.dma_idx[queue] % NRING
        self.dma_idx[queue] += 1
        key = (queue, slot)
        uses = self.dma_uses.get(key, 0)
        deps = self._deps(reads, writes)
        if uses > 0:
            deps[key] = max(deps.get(key, 0), 16 * uses)
        waits = self._waits(queue, deps)
        val = 16 * (uses + 1)
        self.dma_uses[key] = uses + 1
        self.q[queue].append((waits, fn, key))
        for t in reads:
            t.r[key] = val
        for t in writes:
            t.w = (key, val)
            t.r = {}
        if self.tick is not None:
            self.tick()


def co_run(P, tasks):
    if len(tasks) == 1:
        tasks[0][0]()
        return
    n = len(tasks)
    st = {'turn': 0, 'done': [False] * n, 'err': None}
    cv = threading.Condition()
    tl = threading.local()
    left = [t[1] for t in tasks]

    def nxt(i):
        for k in range(1, n + 1):
            j = (i + k) % n
            if not st['done'][j]:
                return j
        return -1

    def tick():
        i = tl.idx
        left[i] -= 1
        if left[i] > 0:
            return
        left[i] = tasks[i][1]
        with cv:
            j = nxt(i)
            if j == i or j < 0:
                return
            st['turn'] = j
            cv.notify_all()
            while st['turn'] != i:
                cv.wait()

    def worker(i):
        tl.idx = i
        _TL.bank_subset = tasks[i][2] if len(tasks[i]) > 2 else None
        _TL.bank_i = 0
        with cv:
            while st['turn'] != i:
                cv.wait()
        try:
            tasks[i][0]()
        except BaseException as e:
            st['err'] = e
        finally:
            with cv:
                st['done'][i] = True
                st['turn'] = nxt(i)
                cv.notify_all()

    P.tick = tick
    ths = [threading.Thread(target=worker, args=(i,)) for i in range(n)]
    for t in ths:
        t.start()
    for t in ths:
        t.join()
    P.tick = None
    if st['err'] is not None:
        raise st['err']


def act_fn(out, in_, func, **kw):
    return lambda e: e.activation(out=out, in_=in_, func=func, **kw)


def tt_fn(out, a, b, op):
    return lambda e: e.tensor_tensor(out=out, in0=a, in1=b, op=op)


def ts_fn(out, a, s1, s2, op0, op1=None):
    if op1 is None:
        return lambda e: e.tensor_scalar(out=out, in0=a, scalar1=s1, scalar2=None, op0=op0)
    return lambda e: e.tensor_scalar(out=out, in0=a, scalar1=s1, scalar2=s2, op0=op0, op1=op1)


def stt_fn(out, a, s, b, op0, op1):
    return lambda e: e.scalar_tensor_tensor(out=out, in0=a, scalar=s, in1=b, op0=op0, op1=op1)


def cp_fn(out, in_):
    return lambda e: e.tensor_copy(out=out, in_=in_)


def rcp_fn(out, in_):
    return lambda e: e.reciprocal(out=out, in_=in_)


def mm_fn(out, l, r, st, sp):
    return lambda e: e.matmul(out, l, r, start=st, stop=sp)


def dma_fn(out, in_, **kw):
    return lambda e: e.dma_start(out=out, in_=in_, **kw)


class StopEmit(Exception):
    pass


def build(depth=L, nblk=NBLK, stop=99):
    nc = bass.Bass("TRN2", target_bir_lowering=False)
    P = Prog()
    es = ExitStack()

    def din(name, shape):
        return nc.dram_tensor(name, shape, F32, kind="ExternalInput").ap()

    def dout(name, shape):
        return nc.dram_tensor(name, shape, F32, kind="ExternalOutput").ap()

    x_p = din("x_p", [2048, D])
    x_s = din("x_s", [NS, D])
    sg = din("sg", [L, NS, H, 128, 128])
    sc = din("sc", [L, NS, 3, 3072])
    pre_norm = din("pre_norm", [L, D])
    w_in = din("w_in", [L, D, IN_DIM])
    gmlp_norm = din("gmlp_norm", [L, D])
    w_spatial = din("w_spatial", [L, 8, 128, 128])
    b_spatial = din("b_spatial", [L, 8, 128])
    conv_w = din("conv_w", [L, 4, 3072])
    a_log = din("a_log", [L, H])
    dt_bias = din("dt_bias", [L, H])
    gdn_norm = din("gdn_norm", [L, 128])
    w_proj_a = din("w_proj_a", [L, D, D])
    w_proj_b = din("w_proj_b", [L, D, D])
    w_out = din("w_out", [L, D, D])
    post_norm = din("post_norm", [L, D])
    consts = din("consts", [128, 6 * 128])

    y_p = dout("y_p", [2048, D])
    y_s = dout("y_s", [NS, D])
    gdn_p = dout("gdn_p", [L, H, 128, 128])
    conv_p = dout("conv_p", [L, 3, 3072])
    gdn_s = dout("gdn_s", [L, NS, H, 128, 128])
    conv_s = dout("conv_s", [L, NS, 3, 3072])
    vrow_s = dout("vrow_s", [L, NS, D])

    def sb(name, shape, dtype=F32):
        return es.enter_context(nc.sbuf_tensor(name, shape, dtype))

    x_tm = sb("x_tm", [128, 17, D])
    wsl = sb("wsl", [128, NSLOT, KC, 512], BF16)
    hT = sb("hT", [128, KC, TOK], BF16)
    yaT = sb("yaT", [128, KC, TOK], BF16)
    ybT = sb("ybT", [128, KC, TOK], BF16)
    mg = sb("mg", [128, KC, TOK], BF16)
    vn_s = sb("vn_s", [128, D], BF16)
    gslot = sb("gslot", [128, D])
    gmlp_b = gslot
    bsp_b = sb("bsp_b", [128, 2, 128])
    wsT = sb("wsT", [128, 8, 128], BF16)
    gdn_b = sb("gdn_b", [128, 128])
    smallp = sb("smallp", [128, 64])
    convw = sb("convw", [128, 1, 24, 4])
    cst = sb("cst", [128, 6, 128])
    cst_bf = sb("cst_bf", [128, 2, 128], BF16)
    epsb = sb("epsb", [128, 1])
    idr = sb("idr", [128, 128], F32R)
    tmpA = sb("tmpA", [128, 2 * TOK])
    jh = sb("jh", [128, D], BF16)
    pre = sb("pre", [128, 1, 3 + TB])
    tails = sb("tails", [128, 24, 3])
    acc = tmpA[:, 0:TOK]
    sq = jh[:, 0:TOK]
    rb = tmpA[:, TOK:2 * TOK]
    qn2 = sb("qn2", [128, 3, TOK], BF16)
    kn2 = sb("kn2", [128, 3, TOK], BF16)
    vT2 = sb("vT2", [128, 3, TOK], BF16)
    szb2 = sb("szb2", [128, 3, TOK])
    S = sb("S", [128, H, 128])
    S_bf = sb("S_bf", [128, H, 128], BF16)
    scal = sb("scal", [128, 16, 16])
    st0 = sb("st0", [128, 8])
    stp = sb("stp", [128, 5, 8])
    dl = sb("dl", [128, 3, 128])
    sst_in = sb("sst_in", [128, 4, 128])
    sst_out = sb("sst_out", [128, 2, 128])
    cstage = sb("cstage", [128, 3, 128])
    stt = sb("stt", [128, 8, NS])
    stm = sb("stm", [128, 4, 128])
    stmb = sb("stmb", [128, 2, 128], BF16)
    spad = sb("spad", [128, 4, 128], BF16)
    nm = sb("nm", [128, 1, 4, 128])
    nmr = sb("nmr", [128, 3, 4, 128], F32R)
    vt2 = sb("vt2", [128, 2, 4, 128])
    dlb2 = sb("dlb2", [128, 2, 12, 128], BF16)
    dlc = sb("dlc", [128, 3, 128], BF16)
    sc2 = sb("sc2", [128, 4, 5, 8])

    banks = [es.enter_context(nc.psum_tensor("ps%d" % i, [128, 512], F32)) for i in range(8)]
    bank_t = toks(8)
    bank_i = [0]

    def bank():
        sub = getattr(_TL, 'bank_subset', None)
        if sub:
            i = sub[_TL.bank_i % len(sub)]
            _TL.bank_i += 1
        else:
            i = bank_i[0] % 8
            bank_i[0] += 1
        return banks[i], bank_t[i]

    x_t = toks(17)
    w_t = [toks(4) for _ in range(NSLOT)]
    hT_t = toks(5)
    ya_t = toks(9)
    yb_t = toks(9)
    mg_t = toks(8)
    vn_t = toks(4)
    for a in vn_t:
        a.al = list(mg_t)
    for a in mg_t:
        a.al = list(vn_t)
    mg_s_t = Tok()
    mg_s_t.al = list(vn_t)
    for a in vn_t:
        a.al.append(mg_s_t)
    vns_t, gslot_t, wsT_t, gdn_t, smallp_t, convw_t, cst_t = toks(7)
    bsp2_t = toks(2)
    gmlp_t = gslot_t
    tmpA_t, jh_t, tails_t = toks(3)
    acc_t = tmpA_t
    sq_t = jh_t
    rb_t = tmpA_t
    qn2_t, kn2_t, vT2_t, szb2_t = toks(3), toks(3), toks(3), toks(3)
    pre_t = toks(2)
    pres_t = toks(2)
    S_t = toks(H)
    Sb_t = toks(H)
    scal_t = toks(16)
    st0_t = Tok()
    st0s_t = Tok()
    stp_t = toks(5)
    tmpA_h = toks(2)
    for a in tmpA_h:
        a.al = [tmpA_t]
    tmpA_t.al = list(tmpA_h)
    dl_t = toks(8)
    ssi_t = toks(4)
    sso_t = toks(2)
    cstage_t = Tok()
    stt_t = toks(8)
    stm_t = toks(4)
    stmb_t = toks(2)
    spad_t = toks(4)
    nm_t = toks(16)
    vt2_t = [toks(4), toks(4)]
    dlb2_t = [toks(12), toks(12)]
    dlc_t = toks(3)
    sc2_t = toks(4)

    vn = mg[:].rearrange("p a b -> p (a b)")[:, 0:4096].rearrange("p (a b) -> p a b", a=4)
    I_f = cst[:, 0, :]
    U_f = cst[:, 1, :]
    SU_f = cst[:, 2, :]
    NM_f = cst[:, 3, :]
    ONES_f = cst[:, 4, :]
    TRIL_f = cst[:, 5, :]
    I_b = cst_bf[:, 0, :]
    ONES_b = cst_bf[:, 1, :]

    ACT = lambda f, r, w: P.emit('act', f, r, w)
    DVE = lambda f, r, w: P.emit('dve', f, r, w)

    def MM(out, l, r, st, sp, reads, writes):
        P.emit('pe', mm_fn(out, l, r, st, sp), reads, writes, signal=sp)

    def LD(out, in_, reads, writes, **kw):
        P.dma('sp', dma_fn(out, in_, **kw), reads, writes)

    LD(cst[:], consts.rearrange("p (a b) -> p a b", a=6), [], [cst_t])
    DVE(cp_fn(cst_bf[:, 0, :], cst[:, 0, :]), [cst_t], [cst_t])
    DVE(cp_fn(cst_bf[:, 1, :], cst[:, 4, :]), [cst_t], [cst_t])
    DVE(lambda e: e.memset(epsb[:], EPS), [], [cst_t])
    DVE(cp_fn(idr[:], cst[:, 0, :]), [cst_t], [cst_t])
    DVE(lambda e: e.memset(spad[:], 0.0), [], spad_t)
    DVE(lambda e: e.memset(cstage[:], 0.0), [], [cstage_t])
    for ti in range(16):
        LD(x_tm[:, ti, :], x_p[ti * 128:(ti + 1) * 128, :], [], [x_t[ti]])
    LD(x_tm[0:NS, 16, :], x_s[:, :], [], [x_t[16]])

    wi = [0]

    def wview(wap, l):
        return wap[l].rearrange("(kc p) n -> p kc n", p=128)

    def load_w(parts):
        s = wi[0] % NSLOT
        wi[0] += 1
        for (view, c0, ncols, dc) in parts:
            tk = [w_t[s][qq] for qq in range(4) if dc < (qq + 1) * 128 and dc + ncols > qq * 128]
            P.dma('pool', dma_fn(wsl[:, s, :, dc:dc + ncols], view[:, :, c0:c0 + ncols]), [], tk)
        return wsl[:, s], w_t[s]

    def rsqrt_col(dst, src, n, rows, scale, rt, wt):
        ACT(act_fn(dst, src, AF.Ln, bias=epsb[0:rows, 0:1], scale=scale), rt + [cst_t], wt)
        ACT(act_fn(dst, dst, AF.Exp, scale=-0.5), wt, wt)

    def ck(level):
        if stop <= level:
            P.stopped = True

    for l in range(depth if stop > 0 else 0):
        win_v = wview(w_in, l)
        wpa_v = wview(w_proj_a, l)
        wpb_v = wview(w_proj_b, l)
        wo_v = wview(w_out, l)
        for i in range(4):
            LD(convw[:, 0, :, i], conv_w[l, i, :].rearrange("(c p) -> p c", p=128), [], [convw_t],
               allow_slow_non_contiguous=True)
        LD(gdn_b[:], gdn_norm[l].partition_broadcast(128), [], [gdn_t])
        LD(smallp[:, 0:8], a_log[l].partition_broadcast(128), [], [smallp_t])
        LD(smallp[:, 8:16], dt_bias[l].partition_broadcast(128), [], [smallp_t])
        LD(smallp[:, 24:32], w_spatial[l, :, 0, 0].partition_broadcast(128), [], [smallp_t], allow_slow_non_contiguous=True)
        LD(smallp[:, 32:40], b_spatial[l, :, 0].partition_broadcast(128), [], [smallp_t], allow_slow_non_contiguous=True)
        ACT(act_fn(smallp[:, 16:24], smallp[:, 0:8], AF.Exp), [smallp_t], [smallp_t])
        DVE(ts_fn(smallp[:, 16:24], smallp[:, 16:24], -1.0, None, ALU.mult), [smallp_t], [smallp_t])
        ws_nat = tmpA[:, 0:1024].rearrange("p (g s) -> p g s", g=8)
        LD(ws_nat, w_spatial[l].rearrange("g t s -> t g s"), [], [tmpA_t])
        for g in range(8):
            DVE(tt_fn(dlb2[:, 0, g, :], ws_nat[:, g, :], TRIL_f, ALU.mult), [tmpA_t, cst_t], [dlb2_t[0][g]])
        for hf in range(2):
            bk, bt = bank()
            for g4 in range(4):
                g = hf * 4 + g4
                MM(bk[:, g4 * 128:(g4 + 1) * 128], dlb2[:, 0, g, :], I_b, True, True, [dlb2_t[0][g], cst_t], [bt])
            DVE(cp_fn(wsT[:, hf * 4:(hf + 1) * 4, :], bk[:].rearrange("p (g t) -> p g t", g=4)), [bt], [wsT_t])
        DVE(lambda e: e.memset(S[:], 0.0), [], S_t)
        DVE(lambda e: e.memset(S_bf[:], 0.0), [], Sb_t)
        DVE(lambda e: e.memset(tails[:], 0.0), [], [tails_t])
        LD(conv_s[l, :, 0:2, :], sc[l, :, 1:3, :], [], [])

        for blk in range(nblk):
            has_s = (blk == nblk - 1)
            tiles = [(blk * 4 + i, 128, i * 128, i) for i in range(4)]
            if has_s:
                tiles.append((16, NS, TB, 4))
            ntok = TOK if has_s else TB
            LD(gslot[:], pre_norm[l].partition_broadcast(128), [], [gslot_t])
            for (ti, rows, c0, ii) in tiles:
                xt = x_tm[0:rows, ti, :]
                hb, hbt = (vn[:, ii, :], vn_t[ii]) if ii < 4 else (jh[0:rows, :], jh_t)
                sp_, spt = stp[0:rows, ii, :], stp_t[ii]
                ACT(act_fn(jh[0:rows, :], xt, AF.Square, accum_out=sp_[:, 0:1]), [x_t[ti]], [jh_t, spt])
                rsqrt_col(sp_[:, 1:2], sp_[:, 0:1], 1, rows, 1.0 / D, [spt], [spt])
                DVE(stt_fn(hb[0:rows, :], xt, sp_[:, 1:2], gslot[0:rows, :], ALU.mult, ALU.mult),
                    [x_t[ti], spt, gslot_t], [hbt])
                for hf in range(2):
                    bk, bt = bank()
                    for k4 in range(4):
                        kc = hf * 4 + k4
                        MM(bk[:, k4 * 128:k4 * 128 + rows], hb[0:rows, kc * 128:(kc + 1) * 128],
                           I_b[0:rows, 0:rows], True, True, [hbt, cst_t], [bt])
                    src = bk[:].rearrange("p (k t) -> p k t", k=4)[:, :, 0:rows]
                    eng = ACT if hf == 0 else DVE
                    eng(cp_fn(hT[:, hf * 4:(hf + 1) * 4, c0:c0 + rows], src) if hf else
                        act_fn(hT[:, hf * 4:(hf + 1) * 4, c0:c0 + rows], src, AF.Copy), [bt], [hT_t[ii]])
            ck(1)

            LD(gslot[:], gmlp_norm[l].partition_broadcast(128), [], [gslot_t])
            wv = [load_w([(win_v, C_V + hf * 512, 512, 0)]) for hf in range(2)]
            for (ti, rows, c0, ii) in tiles:
                bks = []
                for hf in range(2):
                    bk, bt = bank()
                    wa, wt_ = wv[hf]
                    for kc in range(KC):
                        MM(bk[0:rows, :], hT[:, kc, c0:c0 + rows], wa[:, kc, :], kc == 0, kc == KC - 1,
                           [hT_t[ii]] + wt_, [bt])
                    ACT(act_fn(jh[0:rows, 0:512], bk[0:rows, :], AF.Square, accum_out=stp[0:rows, ii, 2 + hf:3 + hf]),
                        [bt], [jh_t, stp_t[ii]])
                    bks.append((bk, bt))
                DVE(tt_fn(stp[0:rows, ii, 4:5], stp[0:rows, ii, 2:3], stp[0:rows, ii, 3:4], ALU.add), [stp_t[ii]], [stp_t[ii]])
                rsqrt_col(stp[0:rows, ii, 5:6], stp[0:rows, ii, 4:5], 1, rows, 1.0 / D, [stp_t[ii]], [stp_t[ii]])
                for hf in range(2):
                    bk, bt = bks[hf]
                    if ii < 4:
                        DVE(stt_fn(vn[:, ii, hf * 512:(hf + 1) * 512], bk[:, :], stp[:, ii, 5:6],
                                   gmlp_b[:, hf * 512:(hf + 1) * 512], ALU.mult, ALU.mult),
                            [bt, stp_t[ii], gmlp_t], [vn_t[ii]])
                    else:
                        vs32 = tmpA[0:NS, 0:1024]
                        DVE(stt_fn(vs32[:, hf * 512:(hf + 1) * 512], bk[0:NS, :], stp[0:NS, ii, 5:6],
                                   gmlp_b[0:NS, hf * 512:(hf + 1) * 512], ALU.mult, ALU.mult),
                            [bt, stp_t[ii], gmlp_t], [tmpA_t])
                if ii == 4:
                    vs32 = tmpA[0:NS, 0:1024]
                    P.dma('sp', dma_fn(vrow_s[l, :, :], vs32), [tmpA_t], [])
                    DVE(cp_fn(vn_s[0:NS, :], vs32), [tmpA_t], [vns_t])
                    bk, bt = bank()
                    for g in range(8):
                        MM(bk[:, g * NS:(g + 1) * NS], vn_s[0:NS, g * 128:(g + 1) * 128], I_b[0:NS, 0:NS],
                           True, True, [vns_t, cst_t], [bt])
                    for g in range(8):
                        DVE(ts_fn(stm[:, 3, g * NS:(g + 1) * NS], bk[:, g * NS:(g + 1) * NS],
                                  smallp[:, 24 + g:25 + g], smallp[:, 32 + g:33 + g], ALU.mult, ALU.add),
                            [bt, smallp_t], [stm_t[3]])

            ck(2)
            for gh in range(2):
                wu = load_w([(win_v, C_U + gh * 512, 512, 0)])
                wz = load_w([(win_v, C_Z + gh * 512, 512, 0)])
                for g4 in range(4):
                    g = gh * 4 + g4
                    LD(bsp_b[:, g % 2, :], b_spatial[l, g].partition_broadcast(128), [], [bsp2_t[g % 2]])
                    bu, but = bank()
                    bz, bzt = bank()
                    bm, bmt = bank()
                    for kc in range(KC):
                        MM(bu[:, :], wu[0][:, kc, g4 * 128:(g4 + 1) * 128], hT[:, kc, 0:TB], kc == 0, kc == KC - 1,
                           hT_t[0:4] + wu[1], [but])
                    for kc in range(KC):
                        MM(bz[:, :], wz[0][:, kc, g4 * 128:(g4 + 1) * 128], hT[:, kc, 0:TB], kc == 0, kc == KC - 1,
                           hT_t[0:4] + wz[1], [bzt])
                    for i in range(4):
                        MM(bm[:, i * 128:(i + 1) * 128], vn[:, i, g * 128:(g + 1) * 128], wsT[:, g, :], True, True,
                           [vn_t[i], wsT_t], [bmt])
                    t_sz = tmpA[:, 0:TB]
                    t_1 = tmpA[:, TOK:TOK + TB]
                    ACT(act_fn(t_sz, bz[:, :], AF.Silu), [bzt], [tmpA_t])
                    DVE(tt_fn(t_1.rearrange("p (i t) -> p i t", i=4), bm[:].rearrange("p (i t) -> p i t", i=4),
                              bsp_b[:, g % 2, :].unsqueeze(1).to_broadcast([128, 4, 128]), ALU.add),
                        [bmt, bsp2_t[g % 2]], [tmpA_t])
                    DVE(tt_fn(t_1, t_1, t_sz, ALU.mult), [tmpA_t], [tmpA_t])
                    DVE(tt_fn(yaT[:, g, 0:TB], t_1, bu[:, :], ALU.mult), [tmpA_t, but], [ya_t[g]])
                    if has_s:
                        bs, bst = bank()
                        for kc in range(KC):
                            MM(bs[:, 0:NS], wu[0][:, kc, g4 * 128:(g4 + 1) * 128], hT[:, kc, TB:TOK], kc == 0,
                               kc == KC - 1, [hT_t[4]] + wu[1], [bst])
                        for kc in range(KC):
                            MM(bs[:, NS:2 * NS], wz[0][:, kc, g4 * 128:(g4 + 1) * 128], hT[:, kc, TB:TOK], kc == 0,
                               kc == KC - 1, [hT_t[4]] + wz[1], [bst])
                        ACT(act_fn(stt[:, 0, :], bs[:, NS:2 * NS], AF.Silu), [bst], [stt_t[0]])
                        DVE(tt_fn(stt[:, 0, :], stt[:, 0, :], stm[:, 3, g * NS:(g + 1) * NS], ALU.mult),
                            [stt_t[0], stm_t[3]], [stt_t[0]])
                        DVE(tt_fn(yaT[:, g, TB:TOK], stt[:, 0, :], bs[:, 0:NS], ALU.mult), [stt_t[0], bst], [ya_t[8]])

            ck(3)
            wab = load_w([(win_v, C_A, 16, 0)])
            for (ti, rows, c0, ii) in tiles:
                ck(3.01 + 0.01 * ii)
                bk, bt = bank()
                for kc in range(KC):
                    MM(bk[0:rows, 0:16], hT[:, kc, c0:c0 + rows], wab[0][:, kc, 0:16], kc == 0, kc == KC - 1,
                       [hT_t[ii]] + wab[1], [bt])
                ACT(act_fn(scal[0:rows, ii, 0:8], bk[0:rows, 8:16], AF.Sigmoid), [bt], [scal_t[ii]])
                DVE(tt_fn(scal[0:rows, 5 + ii, 0:8], bk[0:rows, 0:8], smallp[0:rows, 8:16], ALU.add),
                    [bt, smallp_t, scal_t[ii]], [scal_t[5 + ii]])
                ACT(act_fn(scal[0:rows, 5 + ii, 0:8], scal[0:rows, 5 + ii, 0:8], AF.Exp), [scal_t[5 + ii]],
                    [scal_t[5 + ii]])
                ACT(act_fn(scal[0:rows, 5 + ii, 0:8], scal[0:rows, 5 + ii, 0:8], AF.Ln, bias=ONES_f[0:rows, 0:1], scale=1.0),
                    [scal_t[5 + ii], cst_t], [scal_t[5 + ii]])
                DVE(tt_fn(scal[0:rows, 5 + ii, 0:8], scal[0:rows, 5 + ii, 0:8], smallp[0:rows, 16:24], ALU.mult),
                    [scal_t[5 + ii], smallp_t], [scal_t[5 + ii]])
                ck(3.015 + 0.01 * ii)
                if ii < 4:
                    bk2, bt2 = bank()
                    MM(bk2[:, 0:8], U_f, scal[:, 5 + ii, 0:8], True, True, [cst_t, scal_t[5 + ii]], [bt2])
                    MM(bk2[:, 8:16], ONES_f, scal[:, 5 + ii, 0:8], True, True, [cst_t, scal_t[5 + ii]], [bt2])
                    DVE(cp_fn(scal[:, 10 + ii, 0:16], bk2[:, 0:16]), [bt2], [scal_t[10 + ii]])
                    gc8, gl8 = scal[:, 10 + ii, 0:8], scal[:, 10 + ii, 8:16]
                    sA, sB = [scal_t[10 + ii]], [sc2_t[ii]]
                    ACT(act_fn(sc2[:, ii, 0, :], gc8, AF.Exp), sA, sB)
                    DVE(tt_fn(sc2[:, ii, 1, :], gl8, gc8, ALU.subtract), sA, sB)
                    ACT(act_fn(sc2[:, ii, 1, :], sc2[:, ii, 1, :], AF.Exp), sB, sB)
                    ACT(act_fn(sc2[:, ii, 2, :], gl8, AF.Exp), sA, sB)
                    DVE(ts_fn(sc2[:, ii, 3, :], sc2[:, ii, 0, :], -1.0, None, ALU.mult), sB, sB)
                    DVE(ts_fn(sc2[:, ii, 4, :], gc8, -1.0, None, ALU.mult), sA, sB)
                else:
                    ACT(act_fn(scal[0:NS, 5 + ii, 0:8], scal[0:NS, 5 + ii, 0:8], AF.Exp), [scal_t[5 + ii]], [scal_t[5 + ii]])
                    for which, srcsl in ((0, scal[0:NS, ii, 0:8]), (1, scal[0:NS, 5 + ii, 0:8])):
                        DVE(lambda e, which=which: e.memset(stm[:, which, :], 0.0), [], [stm_t[which]])
                        for s2 in range(NS):
                            DVE(ts_fn(stm[0:NS, which, s2 * 8:(s2 + 1) * 8], srcsl, I_f[0:NS, s2:s2 + 1], None, ALU.mult),
                                [scal_t[ii], scal_t[5 + ii], cst_t], [stm_t[which]])
                    ck(3.07)
                    bk2, bt2 = bank()
                    MM(bk2[:, 0:128], ONES_f, stm[:, 0, :], True, True, [cst_t, stm_t[0]], [bt2])
                    MM(bk2[:, 128:256], ONES_f, stm[:, 1, :], True, True, [cst_t, stm_t[1]], [bt2])
                    ck(3.08)
                    DVE(cp_fn(stm[:, 0, :], bk2[:, 0:128]), [bt2], [stm_t[0]])
                    DVE(cp_fn(stm[:, 1, :], bk2[:, 128:256]), [bt2], [stm_t[1]])

            ck(3.1)
            def front(h, bs, bs2=None):
                qn, kn, vT, szbT = qn2[:, bs, :], kn2[:, bs, :], vT2[:, bs, :], szb2[:, bs, :]
                qn_t, kn_t, vT_t, szb_t = qn2_t[bs], kn2_t[bs], vT2_t[bs], szb2_t[bs]
                wh = load_w([(win_v, C_Q + h * 128, 128, 0), (win_v, C_K + h * 128, 128, 128),
                             (win_v, C_VV + h * 128, 128, 256), (win_v, C_ZB + h * 128, 128, 384)])
                if has_s:
                    bk, bt = bank()
                    for kc in range(KC):
                        MM(bk[0:NS, 0:384], hT[:, kc, TB:TOK], wh[0][:, kc, 0:384], kc == 0, kc == KC - 1,
                           [hT_t[4]] + wh[1], [bt])
                    ACT(act_fn(cstage[0:NS, :, :].rearrange("p a b -> p (a b)"), bk[0:NS, 0:384], AF.Copy), [bt], [cstage_t])
                    for j in range(3):
                        c0_ = j * 1024 + h * 128
                        P.dma('sp', dma_fn(conv_s[l, :, 2, c0_:c0_ + 128], cstage[0:NS, j, :]), [cstage_t], [])
                for j, (dst, dtok) in enumerate(((qn, qn_t), (kn, kn_t), (vT, vT_t))):
                    ch = j * 8 + h
                    pb = 0
                    bk, bt = bank()
                    for kc in range(KC):
                        MM(bk[:, :], wh[0][:, kc, j * 128:(j + 1) * 128], hT[:, kc, 0:TB], kc == 0, kc == KC - 1,
                           hT_t[0:4] + wh[1], [bt])
                    pj = pre[:, pb, :]
                    ACT(act_fn(pj[:, 3:3 + TB], bk[:, :], AF.Copy), [bt], [pre_t[pb]])
                    DVE(cp_fn(pj[:, 0:3], tails[:, ch, :]), [tails_t], [pre_t[pb]])
                    DVE(cp_fn(tails[:, ch, :], pj[:, TB:TB + 3]), [pre_t[pb]], [tails_t])
                    if blk == nblk - 1:
                        P.dma('sp', dma_fn(conv_p[l, :, ch * 128:(ch + 1) * 128].rearrange("i c -> c i"),
                                           tails[:, ch, :], allow_slow_non_contiguous=True), [tails_t], [])
                    cw = convw[:, 0, ch, :]
                    DVE(ts_fn(acc[:, 0:TB], pj[:, 0:TB], cw[:, 0:1], None, ALU.mult), [pre_t[pb], convw_t], [acc_t])
                    for i in range(1, 4):
                        DVE(stt_fn(acc[:, 0:TB], pj[:, i:i + TB], cw[:, i:i + 1], acc[:, 0:TB], ALU.mult, ALU.add),
                            [pre_t[pb], convw_t, acc_t], [acc_t])
                    if has_s:
                        P.dma('sp', dma_fn(cstage[0:NS, :, :], sc[l, :, :, ch * 128:(ch + 1) * 128]), [], [cstage_t])
                        bs, bst = bank()
                        for i in range(3):
                            MM(bs[:, i * NS:(i + 1) * NS], cstage[:, i, :], I_f[:, 0:NS], True, True,
                               [cstage_t, cst_t], [bst])
                        for kc in range(KC):
                            MM(bs[:, 3 * NS:4 * NS], wh[0][:, kc, j * 128:(j + 1) * 128], hT[:, kc, TB:TOK],
                               kc == 0, kc == KC - 1, [hT_t[4]] + wh[1], [bst])
                        DVE(ts_fn(acc[:, TB:TOK], bs[:, 0:NS], cw[:, 0:1], None, ALU.mult), [bst, convw_t], [acc_t])
                        for i in range(1, 4):
                            DVE(stt_fn(acc[:, TB:TOK], bs[:, i * NS:(i + 1) * NS], cw[:, i:i + 1], acc[:, TB:TOK],
                                       ALU.mult, ALU.add), [bst, convw_t, acc_t], [acc_t])
                    if j < 2:
                        ACT(act_fn(acc[:, 0:ntok], acc[:, 0:ntok], AF.Silu), [acc_t], [acc_t])
                        ACT(act_fn(sq[:, 0:ntok], acc[:, 0:ntok], AF.Square), [acc_t], [sq_t])
                        bk2, bt2 = bank()
                        MM(bk2[:, 0:TB], ONES_b, sq[:, 0:TB], True, True, [cst_t, sq_t], [bt2])
                        rsqrt_col(rb[:, 0:TB], bk2[:, 0:TB], TB, 128, 1.0, [bt2], [rb_t])
                        if has_s:
                            bk3, bt3 = bank()
                            MM(bk3[:, 0:NS], ONES_b, sq[:, TB:TOK], True, True, [cst_t, sq_t], [bt3])
                            rsqrt_col(rb[:, TB:TOK], bk3[:, 0:NS], NS, 128, 1.0, [bt3], [rb_t])
                        sc_ = (128.0 ** -0.5) if j == 0 else 1.0
                        DVE(stt_fn(dst[:, 0:ntok], acc[:, 0:ntok], sc_, rb[:, 0:ntok], ALU.mult, ALU.mult),
                            [acc_t, rb_t], [dtok])
                    else:
                        ACT(act_fn(dst[:, 0:ntok], acc[:, 0:ntok], AF.Silu), [acc_t], [dtok])
                bk, bt = bank()
                for kc in range(KC):
                    MM(bk[:, :], wh[0][:, kc, 384:512], hT[:, kc, 0:TB], kc == 0, kc == KC - 1,
                       hT_t[0:4] + wh[1], [bt])
                ACT(act_fn(szbT[:, 0:TB], bk[:, :], AF.Silu), [bt], [szb_t])
                if has_s:
                    bs, bst = bank()
                    for kc in range(KC):
                        MM(bs[:, 0:NS], wh[0][:, kc, 384:512], hT[:, kc, TB:TOK], kc == 0,
                           kc == KC - 1, [hT_t[4]] + wh[1], [bst])
                    ACT(act_fn(szbT[:, TB:TOK], bs[:, 0:NS], AF.Silu), [bst], [szb_t])

                ck(3.2)
            def deltaA(h, bs3, bs):
                qn, kn, vT, szbT = qn2[:, bs3, :], kn2[:, bs3, :], vT2[:, bs3, :], szb2[:, bs3, :]
                qn_t, kn_t, vT_t, szb_t = qn2_t[bs3], kn2_t[bs3], vT2_t[bs3], szb2_t[bs3]
                c_ = lambda i: slice(i * 128, (i + 1) * 128)
                Qf = lambda i: nmr[:, 0, i, :]
                Lf = lambda i: nmr[:, 1, i, :]
                Mf = lambda i: nmr[:, 2, i, :]
                DT = lambda i: nm[:, 0, i, :]
                VT = lambda i: vt2[:, bs, i, :]
                fl = lambda k: (nm[:, 0, :, :] if k == 3 else nmr[:, k, :, :]).rearrange("p t c -> p (t c)")
                At = lambda i: dlb2[:, bs, i, :]
                kd = lambda i: dlb2[:, bs, 4 + i, :]
                Wt = lambda i: dlb2[:, bs, 8 + i, :]
                dlb_t = dlb2_t[bs]
                nQ, nL, nM, nD, nV = nm_t[0:4], nm_t[4:8], nm_t[8:12], nm_t[12:16], vt2_t[bs]
                bg, bgt = bank()
                for i in range(4):
                    DVE(ts_fn(DT(i), U_f, scal[:, 5 + i, h:h + 1], None, ALU.mult), [cst_t, scal_t[5 + i]], [nD[i]])
                    MM(bg[:, c_(i)], ONES_f, DT(i), True, True, [cst_t, nD[i]], [bgt])
                for i in range(4):
                    DVE(stt_fn(DT(i), bg[:, c_(i)], sc2[:, i, 4, h:h + 1], NM_f, ALU.add, ALU.add),
                        [bgt, cst_t, sc2_t[i]], [nD[i]])
                ACT(act_fn(fl(3), fl(3), AF.Exp), nD, nD)
                bkk, bkkt = bank()
                bat, batt = bank()
                for i in range(4):
                    MM(bkk[:, c_(i)], kn[:, c_(i)], kn[:, c_(i)], True, True, [kn_t], [bkkt])
                for i in range(4):
                    MM(bat[:, c_(i)], kn[:, c_(i)], qn[:, c_(i)], True, True, [kn_t, qn_t], [batt])
                for i in range(4):
                    DVE(stt_fn(Qf(i), bkk[:, c_(i)], scal[:, i, h:h + 1], DT(i), ALU.mult, ALU.mult),
                        [bkkt, nD[i], scal_t[i]], [nQ[i]])
                DVE(tt_fn(dlb2[:, bs, 0:4, :].rearrange("p t c -> p (t c)"), bat[:, :], fl(3), ALU.mult), [batt] + nD,
                    dlb_t[0:4])
                DVE(tt_fn(nmr[:, 0, :, :], nmr[:, 0, :, :], SU_f.unsqueeze(1).to_broadcast([128, 4, 128]), ALU.mult),
                    nQ + [cst_t], nQ)
                DVE(stt_fn(nmr[:, 2, :, :], nmr[:, 0, :, :], -1.0, I_f.unsqueeze(1).to_broadcast([128, 4, 128]),
                           ALU.mult, ALU.add), nQ + [cst_t], nM)
                bl, blt = bank()
                for i in range(4):
                    MM(bl[:, c_(i)], Qf(i), idr[:], True, True, [nQ[i], cst_t], [blt])
                ACT(act_fn(fl(1), bl[:, :], AF.Copy), [blt], nL)
                for lev in range(1, 7):
                    b1, b1t = bank()
                    for i in range(4):
                        MM(b1[:, c_(i)], Qf(i), Lf(i), True, True, [nQ[i], nL[i]], [b1t])
                    if lev < 6:
                        b3, b3t = bank()
                        for i in range(4):
                            MM(b3[:, c_(i)], Lf(i), Qf(i), True, True, [nQ[i], nL[i]], [b3t])
                    ACT(act_fn(fl(1), b1[:, :], AF.Copy), [b1t], nL)
                    if lev < 6:
                        DVE(cp_fn(fl(0), b3[:, :]), [b3t], nQ)
                    b2, b2t = bank()
                    for i in range(4):
                        MM(b2[:, c_(i)], Lf(i), Mf(i), True, True, [nL[i], nM[i]], [b2t])
                    if lev < 6:
                        DVE(tt_fn(fl(2), b2[:, :], fl(2), ALU.add), [b2t] + nM, nM)
                    else:
                        DVE(tt_fn(dlb2[:, bs, 8:12, :].rearrange("p t c -> p (t c)"), b2[:, :], fl(2), ALU.add),
                            [b2t] + nM, dlb_t[8:12])
                btk, btkt = bank()
                btv, btvt = bank()
                for i in range(4):
                    MM(btk[:, c_(i)], kn[:, c_(i)], I_b, True, True, [kn_t, cst_t], [btkt])
                for i in range(4):
                    MM(btv[:, c_(i)], vT[:, c_(i)], I_b, True, True, [vT_t, cst_t], [btvt])
                for i in range(4):
                    ACT(act_fn(kd(i), btk[:, c_(i)], AF.Copy, scale=sc2[:, i, 1, h:h + 1]), [btkt, sc2_t[i]],
                        [dlb_t[4 + i]])
                ACT(act_fn(vt2[:, bs, :, :].rearrange("p t c -> p (t c)"), btv[:, :], AF.Copy), [btvt], nV)
            def deltaB(h, bs3, bs):
                qn, kn, vT, szbT = qn2[:, bs3, :], kn2[:, bs3, :], vT2[:, bs3, :], szb2[:, bs3, :]
                qn_t, kn_t, vT_t, szb_t = qn2_t[bs3], kn2_t[bs3], vT2_t[bs3], szb2_t[bs3]
                c_ = lambda i: slice(i * 128, (i + 1) * 128)
                VT = lambda i: vt2[:, bs, i, :]
                At = lambda i: dlb2[:, bs, i, :]
                kd = lambda i: dlb2[:, bs, 4 + i, :]
                Wt = lambda i: dlb2[:, bs, 8 + i, :]
                dlb_t = dlb2_t[bs]
                nV = vt2_t[bs]
                for i in range(4):
                    cs = c_(i)
                    beta = scal[:, i, h:h + 1]
                    eg, egl, negeg = sc2[:, i, 0, h:h + 1], sc2[:, i, 2, h:h + 1], sc2[:, i, 3, h:h + 1]
                    bp, bpt = bank()
                    MM(bp[:, 0:128], kn[:, cs], S_bf[:, h, :], True, True, [kn_t, Sb_t[h]], [bpt])
                    bp2, bp2t = bank()
                    MM(bp2[:, 0:128], qn[:, cs], S_bf[:, h, :], True, True, [qn_t, Sb_t[h]], [bp2t])
                    Z, Zt = dlc[:, 0, :], dlc_t[0]
                    DVE(stt_fn(Z, bp[:, 0:128], negeg, VT(i), ALU.mult, ALU.add), [bpt, nV[i], sc2_t[i]], [Zt])
                    bv, bvt = bank()
                    MM(bv[:, 0:128], Wt(i), Z, True, True, [dlb_t[8 + i], Zt], [bvt])
                    vnw, vnwt = dlc[:, 1, :], dlc_t[1]
                    ACT(act_fn(vnw, bv[:, 0:128], AF.Copy, scale=beta), [bvt, scal_t[i]], [vnwt])
                    ACT(act_fn(dl[:, 0, :], bp2[:, 0:128], AF.Copy, scale=eg), [bp2t, sc2_t[i]], [dl_t[0]])
                    bo, bot = bank()
                    MM(bo[:, 0:128], At(i), vnw, True, True, [dlb_t[i], vnwt], [bot])
                    MM(bo[:, 128:256], kd(i), vnw, True, True, [dlb_t[4 + i], vnwt], [bot])
                    DVE(stt_fn(S_bf[:, h, :], S[:, h, :], egl, bo[:, 128:256], ALU.mult, ALU.add),
                        [S_t[h], bot, sc2_t[i]], [Sb_t[h]])
                    DVE(stt_fn(S[:, h, :], S[:, h, :], egl, bo[:, 128:256], ALU.mult, ALU.add),
                        [S_t[h], bot, sc2_t[i]], [S_t[h]])
                    DVE(tt_fn(dl[:, 0, :], dl[:, 0, :], bo[:, 0:128], ALU.add), [dl_t[0], bot], [dl_t[0]])
                    ACT(act_fn(dl[:, 1, :], dl[:, 0, :], AF.Square, accum_out=st0[:, 2:3]), [dl_t[0]], [dl_t[1], st0_t])
                    rsqrt_col(st0[:, 3:4], st0[:, 2:3], 1, 128, 1.0 / 128, [st0_t], [st0_t])
                    yb, ybt_ = dlc[:, 2, :], dlc_t[2]
                    DVE(stt_fn(yb, dl[:, 0, :], st0[:, 3:4], gdn_b[:, :], ALU.mult, ALU.mult),
                        [dl_t[0], st0_t, gdn_t], [ybt_])
                    by, byt = bank()
                    MM(by[:, 0:128], yb, I_b, True, True, [ybt_, cst_t], [byt])
                    DVE(tt_fn(ybT[:, h, cs], by[:, 0:128], szbT[:, cs], ALU.mult), [byt, szb_t], [yb_t[h]])
                ck(3.3)
                if blk == nblk - 1:
                    P.dma('sp', dma_fn(gdn_p[l, h, :, :], S[:, h, :]), [S_t[h]], [])

            def deltaS(h, bs3):
                qn, kn, vT, szbT = qn2[:, bs3, :], kn2[:, bs3, :], vT2[:, bs3, :], szb2[:, bs3, :]
                qn_t, kn_t, vT_t, szb_t = qn2_t[bs3], kn2_t[bs3], vT2_t[bs3], szb2_t[bs3]
                if has_s:
                    ss = slice(TB, TOK)
                    beta_b = stm[:, 0, :].rearrange("p (s h) -> p s h", s=NS)[:, :, h]
                    eg_b = stm[:, 1, :].rearrange("p (s h) -> p s h", s=NS)[:, :, h]
                    kq = dl[:, 2, 0:2 * NS].rearrange("p (s two) -> p s two", two=2)
                    DVE(cp_fn(kq[:, :, 0], kn[:, ss]), [kn_t], [dl_t[2]])
                    DVE(cp_fn(kq[:, :, 1], qn[:, ss]), [qn_t], [dl_t[2]])
                    bst_, bstt = bank()
                    for s in range(NS):
                        sl = s % 4
                        if s % 2 == 0:
                            P.dma('sp', dma_fn(sst_in[:, sl:sl + 2, :], sg[l, s:s + 2, h, :, :].rearrange("s k v -> k s v")),
                                  [], [ssi_t[sl], ssi_t[sl + 1]])
                        MM(bst_[:, 2 * s:2 * s + 2], sst_in[:, sl, :], kq[:, s, :], True, True, [ssi_t[sl], dl_t[2]],
                           [bstt])
                    stkq = bst_[:, 0:2 * NS].rearrange("p (s two) -> p s two", two=2)
                    DVE(tt_fn(stt[:, 2, :], stkq[:, :, 0], eg_b, ALU.mult), [bstt, stm_t[1]], [stt_t[2]])
                    DVE(tt_fn(stt[:, 2, :], vT[:, ss], stt[:, 2, :], ALU.subtract), [vT_t, stt_t[2]], [stt_t[2]])
                    DVE(tt_fn(stt[:, 2, :], stt[:, 2, :], beta_b, ALU.mult), [stt_t[2], stm_t[0]], [stt_t[2]])
                    DVE(tt_fn(stt[:, 3, :], kq[:, :, 0], kq[:, :, 1], ALU.mult), [dl_t[2]], [stt_t[3]])
                    bqk, bqkt = bank()
                    MM(bqk[:, 0:NS], ONES_f, stt[:, 3, :], True, True, [cst_t, stt_t[3]], [bqkt])
                    DVE(tt_fn(stt[:, 4, :], stkq[:, :, 1], eg_b, ALU.mult), [bstt, stm_t[1]], [stt_t[4]])
                    DVE(tt_fn(stt[:, 3, :], bqk[:, 0:NS], stt[:, 2, :], ALU.mult), [bqkt, stt_t[2]], [stt_t[3]])
                    DVE(tt_fn(stt[:, 4, :], stt[:, 4, :], stt[:, 3, :], ALU.add), [stt_t[3], stt_t[4]], [stt_t[4]])
                    btp, btpt = bank()
                    MM(btp[0:NS, 0:128], stt[:, 2, :], I_f, True, True, [stt_t[2], cst_t], [btpt])
                    MM(btp[0:NS, 128:256], kq[:, :, 0], I_f, True, True, [dl_t[2], cst_t], [btpt])
                    MM(btp[0:NS, 256:384], stt[:, 4, :], I_f, True, True, [stt_t[4], cst_t], [btpt])
                    DVE(cp_fn(stm[0:NS, 2, :], btp[0:NS, 0:128]), [btpt], [stm_t[2]])
                    DVE(cp_fn(spad[0:NS, 0, :], btp[0:NS, 128:256]), [btpt], [spad_t[0]])
                    ACT(act_fn(stmb[0:NS, 1, :], btp[0:NS, 256:384], AF.Square, accum_out=st0[0:NS, 6:7]),
                        [btpt, stm_t[2], spad_t[0]], [stmb_t[1], st0s_t])
                    rsqrt_col(st0[0:NS, 7:8], st0[0:NS, 6:7], 1, NS, 1.0 / 128, [st0s_t], [st0s_t])
                    DVE(stt_fn(stmb[0:NS, 0, :], btp[0:NS, 256:384], st0[0:NS, 7:8], gdn_b[0:NS, :], ALU.mult, ALU.mult),
                        [btpt, st0s_t, gdn_t], [stmb_t[0]])
                    by, byt = bank()
                    MM(by[:, 0:NS], stmb[0:NS, 0, :], I_b[0:NS, 0:NS], True, True, [stmb_t[0], cst_t], [byt])
                    DVE(tt_fn(ybT[:, h, ss], by[:, 0:NS], szbT[:, ss], ALU.mult), [byt, szb_t], [yb_t[8]])
                    msk = [(spad[0:NS, 1, :], spad[0:NS, 1, :], spad_t[1]), (spad[0:NS, 2, :], spad[0:NS, 2, :], spad_t[2])]
                    for s in (0, 2):
                        P.dma('sp', dma_fn(sst_in[:, s:s + 2, :], sg[l, s:s + 2, h, :, :].rearrange("s k v -> k s v")),
                              [], [ssi_t[s], ssi_t[s + 1]])
                    for s0 in range(0, NS, 2):
                        for u in range(2):
                            s = s0 + u
                            DVE(ts_fn(msk[u][1], stm[0:NS, 2, :], I_f[0:NS, s:s + 1], None, ALU.mult),
                                [stm_t[2], cst_t], [msk[u][2]])
                        bop, bopt = bank()
                        for u in range(2):
                            MM(bop[:, u * 128:(u + 1) * 128], spad[0:NS, 0, :], msk[u][0], True, True,
                               [spad_t[0], msk[u][2]], [bopt])
                        for u in range(2):
                            s = s0 + u
                            sl = s % 4
                            DVE(stt_fn(sst_out[:, u, :], sst_in[:, sl, :], eg_b[:, s:s + 1], bop[:, u * 128:(u + 1) * 128],
                                       ALU.mult, ALU.add), [ssi_t[sl], stm_t[1], bopt], [sso_t[u]])
                        P.dma('sp', dma_fn(gdn_s[l, s0:s0 + 2, h, :, :].rearrange("s k v -> k s v"), sst_out[:, 0:2, :]),
                              [sso_t[0], sso_t[1]], [])
                        s = s0 + 4
                        if s < NS:
                            P.dma('sp', dma_fn(sst_in[:, s % 4:s % 4 + 2, :], sg[l, s:s + 2, h, :, :].rearrange("s k v -> k s v")),
                                  [], [ssi_t[s % 4], ssi_t[s % 4 + 1]])

            BF_, BA_, BB_, BS_ = [0, 1], [2, 3], [4, 5], [6, 7]
            QF_, QA_, QB_, QS_ = (4, 4, 3, 3) if has_s else (8, 8, 4, 3)
            front(0, 0)
            co_run(P, [((lambda: deltaA(0, 0, 0)), QA_, BA_), ((lambda: front(1, 1)), QF_, BF_)])
            for h in range(H):
                tasks = [((lambda h=h: deltaB(h, h % 3, h % 2)), QB_, BB_)]
                if has_s:
                    tasks.append(((lambda h=h: deltaS(h, h % 3)), QS_, BS_))
                if h + 1 < H:
                    tasks.append(((lambda h=h: deltaA(h + 1, (h + 1) % 3, (h + 1) % 2)), QA_, BA_))
                if h + 2 < H:
                    tasks.append(((lambda h=h: front(h + 2, (h + 2) % 3)), QF_, BF_))
                co_run(P, tasks)

            ck(4)
            ybr = yb_t[0:9] if has_s else yb_t[0:8]
            yar = ya_t[0:9] if has_s else ya_t[0:8]
            for qd in range(4):
                wa = load_w([(wpa_v, qd * 256, 256, 0), (win_v, C_GA + qd * 256, 256, 256)])
                wb = load_w([(wpb_v, qd * 256, 256, 0), (win_v, C_GB + qd * 256, 256, 256)])
                for j2 in range(2):
                    j = qd * 2 + j2
                    groups = [(0, TB, hT_t[0:4])] + ([(TB, NS, [hT_t[4]])] if has_s else [])
                    for (c0, n, htk) in groups:
                        bpa, bpat = bank()
                        bpb, bpbt = bank()
                        bga, bgat = bank()
                        bgb, bgbt = bank()
                        for kc in range(KC):
                            MM(bpa[:, 0:n], wa[0][:, kc, j2 * 128:(j2 + 1) * 128], yaT[:, kc, c0:c0 + n], kc == 0,
                               kc == KC - 1, yar + wa[1], [bpat])
                        for kc in range(KC):
                            MM(bpb[:, 0:n], wb[0][:, kc, j2 * 128:(j2 + 1) * 128], ybT[:, kc, c0:c0 + n], kc == 0,
                               kc == KC - 1, ybr + wb[1], [bpbt])
                        for kc in range(KC):
                            MM(bga[:, 0:n], wa[0][:, kc, 256 + j2 * 128:256 + (j2 + 1) * 128], hT[:, kc, c0:c0 + n],
                               kc == 0, kc == KC - 1, htk + wa[1], [bgat])
                        for kc in range(KC):
                            MM(bgb[:, 0:n], wb[0][:, kc, 256 + j2 * 128:256 + (j2 + 1) * 128], hT[:, kc, c0:c0 + n],
                               kc == 0, kc == KC - 1, htk + wb[1], [bgbt])
                        ta = tmpA[:, 0:n]
                        tb_ = tmpA[:, TOK:TOK + n]
                        ACT(act_fn(ta, bga[:, 0:n], AF.Sigmoid), [bgat], [tmpA_t])
                        ACT(act_fn(tb_, bgb[:, 0:n], AF.Sigmoid), [bgbt], [tmpA_t])
                        DVE(tt_fn(ta, ta, bpa[:, 0:n], ALU.mult), [tmpA_t, bpat], [tmpA_t])
                        DVE(tt_fn(tb_, tb_, bpb[:, 0:n], ALU.mult), [tmpA_t, bpbt], [tmpA_t])
                        DVE(tt_fn(mg[:, j, c0:c0 + n], ta, tb_, ALU.add), [tmpA_t], [mg_t[j]] if c0 == 0 else [mg_s_t])

            ck(5)
            LD(gslot[:], post_norm[l].partition_broadcast(128), [], [gslot_t])
            wo = [load_w([(wo_v, hf * 512, 512, 0)]) for hf in range(2)]
            for (ti, rows, c0, ii) in tiles:
                bks = []
                for hf in range(2):
                    bk, bt = bank()
                    for kc in range(KC):
                        MM(bk[0:rows, :], mg[:, kc, c0:c0 + rows], wo[hf][0][:, kc, :], kc == 0, kc == KC - 1,
                           (mg_t if ii < 4 else [mg_s_t]) + wo[hf][1], [bt])
                    ACT(act_fn(jh[0:rows, 0:512], bk[0:rows, :], AF.Square, accum_out=stp[0:rows, ii, 2 + hf:3 + hf]),
                        [bt], [jh_t, stp_t[ii]])
                    bks.append((bk, bt))
                DVE(tt_fn(stp[0:rows, ii, 4:5], stp[0:rows, ii, 2:3], stp[0:rows, ii, 3:4], ALU.add), [stp_t[ii]], [stp_t[ii]])
                rsqrt_col(stp[0:rows, ii, 5:6], stp[0:rows, ii, 4:5], 1, rows, 1.0 / D, [stp_t[ii]], [stp_t[ii]])
                for hf in range(2):
                    bk, bt = bks[hf]
                    t1 = tmpA[0:rows, hf * TOK:hf * TOK + 512]
                    DVE(stt_fn(t1, bk[0:rows, :], stp[0:rows, ii, 5:6], gslot[0:rows, hf * 512:(hf + 1) * 512], ALU.mult,
                               ALU.mult), [bt, stp_t[ii], gslot_t], [tmpA_h[hf]])
                    xs = x_tm[0:rows, ti, hf * 512:(hf + 1) * 512]
                    DVE(tt_fn(xs, xs, t1, ALU.add), [tmpA_h[hf], x_t[ti]], [x_t[ti]])
                if l == depth - 1:
                    if ii < 4:
                        P.dma('sp', dma_fn(y_p[ti * 128:(ti + 1) * 128, :], x_tm[:, ti, :]), [x_t[ti]], [])
                    else:
                        P.dma('sp', dma_fn(y_s[:, :], x_tm[0:NS, 16, :]), [x_t[16]], [])

    sems = {}
    for e in ENGS:
        sems[e] = es.enter_context(nc.semaphore("s_" + e))
    for key in P.dma_uses:
        sems[key] = es.enter_context(nc.semaphore("d_%s_%d" % key))
    final_waits = [(k, 16 * u) for k, u in P.dma_uses.items()] + [(e, P.cnt[e]) for e in ENGS if e != 'sp' and P.cnt[e]]
    P.q['sp'].append((final_waits, None, None))

    def replay(name, e):
        for waits, fn, sig in P.q[name]:
            for k, v in waits:
                e.wait_ge(sems[k], v)
            if fn is None:
                continue
            ins = fn(e)
            if sig is None:
                continue
            if isinstance(sig, tuple):
                ins.then_inc(sems[sig], 16)
            else:
                ins.then_inc(sems[sig], 1)

    with nc.Block() as block:
        @block.tensor
        def _(e):
            replay('pe', e)

        @block.scalar
        def _(e):
            replay('act', e)

        @block.vector
        def _(e):
            replay('dve', e)

        @block.gpsimd
        def _(e):
            replay('pool', e)

        @block.sync
        def _(e):
            replay('sp', e)
    es.close()
    return nc


def make_consts():
    c = np.zeros((128, 6, 128), np.float32)
    idx = np.arange(128)
    c[:, 0, :] = np.eye(128)
    c[:, 1, :] = (idx[:, None] <= idx[None, :])
    c[:, 2, :] = (idx[None, :] > idx[:, None])
    c[:, 3, :] = np.where(idx[None, :] >= idx[:, None], 0.0, -1e30)
    c[:, 4, :] = 1.0
    c[:, 5, :] = (idx[None, :] <= idx[:, None])
    return np.ascontiguousarray(c.reshape(128, 768))


def kernel(x_prompt, x_sample, state_gdn, state_conv, pre_norm, w_in, gmlp_norm, w_spatial, b_spatial, conv_w,
           a_log, dt_bias, gdn_norm, w_proj_a, w_proj_b, w_out, post_norm):
    f = lambda a: np.ascontiguousarray(np.asarray(a, dtype=np.float32))
    nc = build()
    consts = make_consts()
    shared = dict(pre_norm=f(pre_norm), w_in=f(w_in), gmlp_norm=f(gmlp_norm), w_spatial=f(w_spatial),
                  b_spatial=f(b_spatial), conv_w=f(conv_w), a_log=f(a_log), dt_bias=f(dt_bias), gdn_norm=f(gdn_norm),
                  w_proj_a=f(w_proj_a), w_proj_b=f(w_proj_b), w_out=f(w_out), post_norm=f(post_norm), consts=consts)
    in_maps = []
    for c in range(8):
        m = dict(shared)
        m["x_p"] = f(x_prompt[c])
        m["x_s"] = f(x_sample[c * NS:(c + 1) * NS, 0, :])
        m["sg"] = f(state_gdn[:, c * NS:(c + 1) * NS])
        m["sc"] = f(state_conv[:, c * NS:(c + 1) * NS])
        in_maps.append(m)
    res = run_bass_kernel_spmd(nc, in_maps, core_ids=list(range(8)))
    r = res.results
    y_prompt = np.stack([r[c]["y_p"] for c in range(8)], axis=0)
    y_sample = np.concatenate([r[c]["y_s"] for c in range(8)], axis=0)[:, None, :]
    gdn_prompt = np.stack([r[c]["gdn_p"] for c in range(8)], axis=1)
    conv_prompt = np.stack([r[c]["conv_p"] for c in range(8)], axis=1)
    gdn_sample = np.concatenate([r[c]["gdn_s"] for c in range(8)], axis=1)
    conv_sample = np.concatenate([r[c]["conv_s"] for c in range(8)], axis=1)
    vrow = np.concatenate([r[c]["vrow_s"] for c in range(8)], axis=1)[:, :, None, :]
    return (y_prompt.astype(np.float32), y_sample.astype(np.float32), gdn_prompt.astype(np.float32),
            conv_prompt.astype(np.float32), gdn_sample.astype(np.float32), conv_sample.astype(np.float32),
            vrow.astype(np.float32))
```

```python
import threading
import numpy as np
from contextlib import ExitStack
import concourse.bass as bass
import concourse.mybir as mybir
from concourse.bass_utils import run_bass_kernel_spmd

F32 = mybir.dt.float32
BF16 = mybir.dt.bfloat16
F32R = mybir.dt.float32r
AF = mybir.ActivationFunctionType
ALU = mybir.AluOpType

D = 1024
KC = 8
TB = 512
NBLK = 4
NS = 16
L = 4
H = 8
TOK = TB + NS
IN_DIM = 9232
C_U, C_V, C_Z, C_Q, C_K, C_VV, C_ZB, C_A, C_GA, C_GB = 0, 1024, 2048, 3072, 4096, 5120, 6144, 7168, 7184, 8208
EPS = 1e-6
NSLOT = 4
NRING = 24
ENGS = ('pe', 'act', 'dve', 'pool', 'sp')
_TL = threading.local()


class Tok:
    __slots__ = ('w', 'r', 'al')

    def __init__(self):
        self.w = None
        self.r = {}
        self.al = []


def toks(n):
    return [Tok() for _ in range(n)]


class Prog:
    def __init__(self):
        self.q = {e: [] for e in ENGS}
        self.cnt = {e: 0 for e in ENGS}
        self.waited = {e: {} for e in ENGS}
        self.dma_idx = {'sp': 0, 'pool': 0}
        self.dma_uses = {}
        self.stopped = False
        self.tick = None

    def _deps(self, reads, writes):
        deps = {}

        def add(k, v):
            if deps.get(k, 0) < v:
                deps[k] = v

        for t in reads:
            for a in [t] + t.al:
                if a.w is not None:
                    add(*a.w)
        for t in writes:
            for a in [t] + t.al:
                if a.w is not None:
                    add(*a.w)
                for k, v in a.r.items():
                    add(k, v)
        return deps

    def _waits(self, eng, deps):
        waits = []
        for k, v in deps.items():
            if k == eng and eng == 'pe':
                continue
            if self.waited[eng].get(k, 0) >= v:
                continue
            self.waited[eng][k] = v
            waits.append((k, v))
        return waits

    def emit(self, eng, fn, reads=(), writes=(), signal=True):
        if self.stopped:
            return
        deps = self._deps(reads, writes)
        waits = self._waits(eng, deps)
        if signal:
            self.cnt[eng] += 1
            n = self.cnt[eng]
        else:
            n = self.cnt[eng] + 1
        self.q[eng].append((waits, fn, eng if signal else None))
        for t in reads:
            if t.r.get(eng, 0) < n:
                t.r[eng] = n
        for t in writes:
            t.w = (eng, n)
            t.r = {}
        if self.tick is not None:
            self.tick()

    def dma(self, queue, fn, reads=(), writes=()):
        if self.stopped:
            return
        slot = self.dma_idx[queue] % NRING
        self.dma_idx[queue] += 1
        key = (queue, slot)
        uses = self.dma_uses.get(key, 0)
        deps = self._deps(reads, writes)
        if uses > 0:
            deps[key] = max(deps.get(key, 0), 16 * uses)
        waits = self._waits(queue, deps)
        val = 16 * (uses + 1)
        self.dma_uses[key] = uses + 1
        self.q[queue].append((waits, fn, key))
        for t in reads:
            t.r[key] = val
        for t in writes:
            t.w = (key, val)
            t.r = {}
        if self.tick is not None:
            self.tick()


def co_run(P, tasks):
    if len(tasks) == 1:
        tasks[0][0]()
        return
    n = len(tasks)
    st = {'turn': 0, 'done': [False] * n, 'err': None}
    cv = threading.Condition()
    tl = threading.local()
    left = [t[1] for t in tasks]

    def nxt(i):
        for k in range(1, n + 1):
            j = (i + k) % n
            if not st['done'][j]:
                return j
        return -1

    def tick():
        i = tl.idx
        left[i] -= 1
        if left[i] > 0:
            return
        left[i] = tasks[i][1]
        with cv:
            j = nxt(i)
            if j == i or j < 0:
                return
            st['turn'] = j
            cv.notify_all()
            while st['turn'] != i:
                cv.wait()

    def worker(i):
        tl.idx = i
        _TL.bank_subset = tasks[i][2] if len(tasks[i]) > 2 else None
        _TL.bank_i = 0
        with cv:
            while st['turn'] != i:
                cv.wait()
        try:
            tasks[i][0]()
        except BaseException as e:
            st['err'] = e
        finally:
            with cv:
                st['done'][i] = True
                st['turn'] = nxt(i)
                cv.notify_all()

    P.tick = tick
    ths = [threading.Thread(target=worker, args=(i,)) for i in range(n)]
    for t in ths:
        t.start()
    for t in ths:
        t.join()
    P.tick = None
    if st['err'] is not None:
        raise st['err']


def act_fn(out, in_, func, **kw):
    return lambda e: e.activation(out=out, in_=in_, func=func, **kw)


def tt_fn(out, a, b, op):
    return lambda e: e.tensor_tensor(out=out, in0=a, in1=b, op=op)


def ts_fn(out, a, s1, s2, op0, op1=None):
    if op1 is None:
        return lambda e: e.tensor_scalar(out=out, in0=a, scalar1=s1, scalar2=None, op0=op0)
    return lambda e: e.tensor_scalar(out=out, in0=a, scalar1=s1, scalar2=s2, op0=op0, op1=op1)


def stt_fn(out, a, s, b, op0, op1):
    return lambda e: e.scalar_tensor_tensor(out=out, in0=a, scalar=s, in1=b, op0=op0, op1=op1)


def cp_fn(out, in_):
    return lambda e: e.tensor_copy(out=out, in_=in_)


def rcp_fn(out, in_):
    return lambda e: e.reciprocal(out=out, in_=in_)


def mm_fn(out, l, r, st, sp):
    return lambda e: e.matmul(out, l, r, start=st, stop=sp)


def dma_fn(out, in_, **kw):
    return lambda e: e.dma_start(out=out, in_=in_, **kw)


class StopEmit(Exception):
    pass


def build(depth=L, nblk=NBLK, stop=99):
    nc = bass.Bass("TRN2", target_bir_lowering=False)
    P = Prog()
    es = ExitStack()

    def din(name, shape):
        return nc.dram_tensor(name, shape, F32, kind="ExternalInput").ap()

    def dout(name, shape):
        return nc.dram_tensor(name, shape, F32, kind="ExternalOutput").ap()

    x_p = din("x_p", [2048, D])
    x_s = din("x_s", [NS, D])
    sg = din("sg", [L, NS, H, 128, 128])
    sc = din("sc", [L, NS, 3, 3072])
    pre_norm = din("pre_norm", [L, D])
    w_in = din("w_in", [L, D, IN_DIM])
    gmlp_norm = din("gmlp_norm", [L, D])
    w_spatial = din("w_spatial", [L, 8, 128, 128])
    b_spatial = din("b_spatial", [L, 8, 128])
    conv_w = din("conv_w", [L, 4, 3072])
    a_log = din("a_log", [L, H])
    dt_bias = din("dt_bias", [L, H])
    gdn_norm = din("gdn_norm", [L, 128])
    w_proj_a = din("w_proj_a", [L, D, D])
    w_proj_b = din("w_proj_b", [L, D, D])
    w_out = din("w_out", [L, D, D])
    post_norm = din("post_norm", [L, D])
    consts = din("consts", [128, 6 * 128])

    y_p = dout("y_p", [2048, D])
    y_s = dout("y_s", [NS, D])
    gdn_p = dout("gdn_p", [L, H, 128, 128])
    conv_p = dout("conv_p", [L, 3, 3072])
    gdn_s = dout("gdn_s", [L, NS, H, 128, 128])
    conv_s = dout("conv_s", [L, NS, 3, 3072])
    vrow_s = dout("vrow_s", [L, NS, D])

    def sb(name, shape, dtype=F32):
        return es.enter_context(nc.sbuf_tensor(name, shape, dtype))

    x_tm = sb("x_tm", [128, 17, D])
    wsl = sb("wsl", [128, NSLOT, KC, 512], BF16)
    hT = sb("hT", [128, KC, TOK], BF16)
    yaT = sb("yaT", [128, KC, TOK], BF16)
    ybT = sb("ybT", [128, KC, TOK], BF16)
    mg = sb("mg", [128, KC, TOK], BF16)
    vn_s = sb("vn_s", [128, D], BF16)
    gslot = sb("gslot", [128, D])
    gmlp_b = gslot
    bsp_b = sb("bsp_b", [128, 2, 128])
    wsT = sb("wsT", [128, 8, 128], BF16)
    gdn_b = sb("gdn_b", [128, 128])
    smallp = sb("smallp", [128, 64])
    convw = sb("convw", [128, 1, 24, 4])
    cst = sb("cst", [128, 6, 128])
    cst_bf = sb("cst_bf", [128, 2, 128], BF16)
    epsb = sb("epsb", [128, 1])
    idr = sb("idr", [128, 128], F32R)
    tmpA = sb("tmpA", [128, 2 * TOK])
    jh = sb("jh", [128, D], BF16)
    pre = sb("pre", [128, 1, 3 + TB])
    tails = sb("tails", [128, 24, 3])
    acc = tmpA[:, 0:TOK]
    sq = jh[:, 0:TOK]
    rb = tmpA[:, TOK:2 * TOK]
    qn2 = sb("qn2", [128, 3, TOK], BF16)
    kn2 = sb("kn2", [128, 3, TOK], BF16)
    vT2 = sb("vT2", [128, 3, TOK], BF16)
    szb2 = sb("szb2", [128, 3, TOK])
    S = sb("S", [128, H, 128])
    S_bf = sb("S_bf", [128, H, 128], BF16)
    scal = sb("scal", [128, 16, 16])
    st0 = sb("st0", [128, 8])
    stp = sb("stp", [128, 5, 8])
    dl = sb("dl", [128, 3, 128])
    sst_in = sb("sst_in", [128, 4, 128])
    sst_out = sb("sst_out", [128, 2, 128])
    cstage = sb("cstage", [128, 3, 128])
    stt = sb("stt", [128, 8, NS])
    stm = sb("stm", [128, 4, 128])
    stmb = sb("stmb", [128, 2, 128], BF16)
    spad = sb("spad", [128, 4, 128], BF16)
    nm = sb("nm", [128, 1, 4, 128])
    nmr = sb("nmr", [128, 3, 4, 128], F32R)
    vt2 = sb("vt2", [128, 2, 4, 128])
    dlb2 = sb("dlb2", [128, 2, 12, 128], BF16)
    dlc = sb("dlc", [128, 3, 128], BF16)
    sc2 = sb("sc2", [128, 4, 5, 8])

    banks = [es.enter_context(nc.psum_tensor("ps%d" % i, [128, 512], F32)) for i in range(8)]
    bank_t = toks(8)
    bank_i = [0]

    def bank():
        sub = getattr(_TL, 'bank_subset', None)
        if sub:
            i = sub[_TL.bank_i % len(sub)]
            _TL.bank_i += 1
        else:
            i = bank_i[0] % 8
            bank_i[0] += 1
        return banks[i], bank_t[i]

    x_t = toks(17)
    w_t = [toks(4) for _ in range(NSLOT)]
    hT_t = toks(5)
    ya_t = toks(9)
    yb_t = toks(9)
    mg_t = toks(8)
    vn_t = toks(4)
    for a in vn_t:
        a.al = list(mg_t)
    for a in mg_t:
        a.al = list(vn_t)
    mg_s_t = Tok()
    mg_s_t.al = list(vn_t)
    for a in vn_t:
        a.al.append(mg_s_t)
    vns_t, gslot_t, wsT_t, gdn_t, smallp_t, convw_t, cst_t = toks(7)
    bsp2_t = toks(2)
    gmlp_t = gslot_t
    tmpA_t, jh_t, tails_t = toks(3)
    acc_t = tmpA_t
    sq_t = jh_t
    rb_t = tmpA_t
    qn2_t, kn2_t, vT2_t, szb2_t = toks(3), toks(3), toks(3), toks(3)
    pre_t = toks(2)
    pres_t = toks(2)
    S_t = toks(H)
    Sb_t = toks(H)
    scal_t = toks(16)
    st0_t = Tok()
    st0s_t = Tok()
    stp_t = toks(5)
    tmpA_h = toks(2)
    for a in tmpA_h:
        a.al = [tmpA_t]
    tmpA_t.al = list(tmpA_h)
    dl_t = toks(8)
    ssi_t = toks(4)
    sso_t = toks(2)
    cstage_t = Tok()
    stt_t = toks(8)
    stm_t = toks(4)
    stmb_t = toks(2)
    spad_t = toks(4)
    nm_t = toks(16)
    vt2_t = [toks(4), toks(4)]
    dlb2_t = [toks(12), toks(12)]
    dlc_t = toks(3)
    sc2_t = toks(4)

    vn = mg[:].rearrange("p a b -> p (a b)")[:, 0:4096].rearrange("p (a b) -> p a b", a=4)
    I_f = cst[:, 0, :]
    U_f = cst[:, 1, :]
    SU_f = cst[:, 2, :]
    NM_f = cst[:, 3, :]
    ONES_f = cst[:, 4, :]
    TRIL_f = cst[:, 5, :]
    I_b = cst_bf[:, 0, :]
    ONES_b = cst_bf[:, 1, :]

    ACT = lambda f, r, w: P.emit('act', f, r, w)
    DVE = lambda f, r, w: P.emit('dve', f, r, w)

    def MM(out, l, r, st, sp, reads, writes):
        P.emit('pe', mm_fn(out, l, r, st, sp), reads, writes, signal=sp)

    def LD(out, in_, reads, writes, **kw):
        P.dma('sp', dma_fn(out, in_, **kw), reads, writes)

    LD(cst[:], consts.rearrange("p (a b) -> p a b", a=6), [], [cst_t])
    DVE(cp_fn(cst_bf[:, 0, :], cst[:, 0, :]), [cst_t], [cst_t])
    DVE(cp_fn(cst_bf[:, 1, :], cst[:, 4, :]), [cst_t], [cst_t])
    DVE(lambda e: e.memset(epsb[:], EPS), [], [cst_t])
    DVE(cp_fn(idr[:], cst[:, 0, :]), [cst_t], [cst_t])
    DVE(lambda e: e.memset(spad[:], 0.0), [], spad_t)
    DVE(lambda e: e.memset(cstage[:], 0.0), [], [cstage_t])
    for ti in range(16):
        LD(x_tm[:, ti, :], x_p[ti * 128:(ti + 1) * 128, :], [], [x_t[ti]])
    LD(x_tm[0:NS, 16, :], x_s[:, :], [], [x_t[16]])

    wi = [0]

    def wview(wap, l):
        return wap[l].rearrange("(kc p) n -> p kc n", p=128)

    def load_w(parts):
        s = wi[0] % NSLOT
        wi[0] += 1
        for (view, c0, ncols, dc) in parts:
            tk = [w_t[s][qq] for qq in range(4) if dc < (qq + 1) * 128 and dc + ncols > qq * 128]
            P.dma('pool', dma_fn(wsl[:, s, :, dc:dc + ncols], view[:, :, c0:c0 + ncols]), [], tk)
        return wsl[:, s], w_t[s]

    def rsqrt_col(dst, src, n, rows, scale, rt, wt):
        ACT(act_fn(dst, src, AF.Ln, bias=epsb[0:rows, 0:1], scale=scale), rt + [cst_t], wt)
        ACT(act_fn(dst, dst, AF.Exp, scale=-0.5), wt, wt)

    def ck(level):
        if stop <= level:
            P.stopped = True

    for l in range(depth if stop > 0 else 0):
        win_v = wview(w_in, l)
        wpa_v = wview(w_proj_a, l)
        wpb_v = wview(w_proj_b, l)
        wo_v = wview(w_out, l)
        for i in range(4):
            LD(convw[:, 0, :, i], conv_w[l, i, :].rearrange("(c p) -> p c", p=128), [], [convw_t],
               allow_slow_non_contiguous=True)
        LD(gdn_b[:], gdn_norm[l].partition_broadcast(128), [], [gdn_t])
        LD(smallp[:, 0:8], a_log[l].partition_broadcast(128), [], [smallp_t])
        LD(smallp[:, 8:16], dt_bias[l].partition_broadcast(128), [], [smallp_t])
        LD(smallp[:, 24:32], w_spatial[l, :, 0, 0].partition_broadcast(128), [], [smallp_t], allow_slow_non_contiguous=True)
        LD(smallp[:, 32:40], b_spatial[l, :, 0].partition_broadcast(128), [], [smallp_t], allow_slow_non_contiguous=True)
        ACT(act_fn(smallp[:, 16:24], smallp[:, 0:8], AF.Exp), [smallp_t], [smallp_t])
        DVE(ts_fn(smallp[:, 16:24], smallp[:, 16:24], -1.0, None, ALU.mult), [smallp_t], [smallp_t])
        ws_nat = tmpA[:, 0:1024].rearrange("p (g s) -> p g s", g=8)
        LD(ws_nat, w_spatial[l].rearrange("g t s -> t g s"), [], [tmpA_t])
        for g in range(8):
            DVE(tt_fn(dlb2[:, 0, g, :], ws_nat[:, g, :], TRIL_f, ALU.mult), [tmpA_t, cst_t], [dlb2_t[0][g]])
        for hf in range(2):
            bk, bt = bank()
            for g4 in range(4):
                g = hf * 4 + g4
                MM(bk[:, g4 * 128:(g4 + 1) * 128], dlb2[:, 0, g, :], I_b, True, True, [dlb2_t[0][g], cst_t], [bt])
            DVE(cp_fn(wsT[:, hf * 4:(hf + 1) * 4, :], bk[:].rearrange("p (g t) -> p g t", g=4)), [bt], [wsT_t])
        DVE(lambda e: e.memset(S[:], 0.0), [], S_t)
        DVE(lambda e: e.memset(S_bf[:], 0.0), [], Sb_t)
        DVE(lambda e: e.memset(tails[:], 0.0), [], [tails_t])
        LD(conv_s[l, :, 0:2, :], sc[l, :, 1:3, :], [], [])

        for blk in range(nblk):
            has_s = (blk == nblk - 1)
            tiles = [(blk * 4 + i, 128, i * 128, i) for i in range(4)]
            if has_s:
                tiles.append((16, NS, TB, 4))
            ntok = TOK if has_s else TB
            LD(gslot[:], pre_norm[l].partition_broadcast(128), [], [gslot_t])
            for (ti, rows, c0, ii) in tiles:
                xt = x_tm[0:rows, ti, :]
                hb, hbt = (vn[:, ii, :], vn_t[ii]) if ii < 4 else (jh[0:rows, :], jh_t)
                sp_, spt = stp[0:rows, ii, :], stp_t[ii]
                ACT(act_fn(jh[0:rows, :], xt, AF.Square, accum_out=sp_[:, 0:1]), [x_t[ti]], [jh_t, spt])
                rsqrt_col(sp_[:, 1:2], sp_[:, 0:1], 1, rows, 1.0 / D, [spt], [spt])
                DVE(stt_fn(hb[0:rows, :], xt, sp_[:, 1:2], gslot[0:rows, :], ALU.mult, ALU.mult),
                    [x_t[ti], spt, gslot_t], [hbt])
                for hf in range(2):
                    bk, bt = bank()
                    for k4 in range(4):
                        kc = hf * 4 + k4
                        MM(bk[:, k4 * 128:k4 * 128 + rows], hb[0:rows, kc * 128:(kc + 1) * 128],
                           I_b[0:rows, 0:rows], True, True, [hbt, cst_t], [bt])
                    src = bk[:].rearrange("p (k t) -> p k t", k=4)[:, :, 0:rows]
                    eng = ACT if hf == 0 else DVE
                    eng(cp_fn(hT[:, hf * 4:(hf + 1) * 4, c0:c0 + rows], src) if hf else
                        act_fn(hT[:, hf * 4:(hf + 1) * 4, c0:c0 + rows], src, AF.Copy), [bt], [hT_t[ii]])
            ck(1)

            LD(gslot[:], gmlp_norm[l].partition_broadcast(128), [], [gslot_t])
            wv = [load_w([(win_v, C_V + hf * 512, 512, 0)]) for hf in range(2)]
            for (ti, rows, c0, ii) in tiles:
                bks = []
                for hf in range(2):
                    bk, bt = bank()
                    wa, wt_ = wv[hf]
                    for kc in range(KC):
                        MM(bk[0:rows, :], hT[:, kc, c0:c0 + rows], wa[:, kc, :], kc == 0, kc == KC - 1,
                           [hT_t[ii]] + wt_, [bt])
                    ACT(act_fn(jh[0:rows, 0:512], bk[0:rows, :], AF.Square, accum_out=stp[0:rows, ii, 2 + hf:3 + hf]),
                        [bt], [jh_t, stp_t[ii]])
                    bks.append((bk, bt))
                DVE(tt_fn(stp[0:rows, ii, 4:5], stp[0:rows, ii, 2:3], stp[0:rows, ii, 3:4], ALU.add), [stp_t[ii]], [stp_t[ii]])
                rsqrt_col(stp[0:rows, ii, 5:6], stp[0:rows, ii, 4:5], 1, rows, 1.0 / D, [stp_t[ii]], [stp_t[ii]])
                for hf in range(2):
                    bk, bt = bks[hf]
                    if ii < 4:
                        DVE(stt_fn(vn[:, ii, hf * 512:(hf + 1) * 512], bk[:, :], stp[:, ii, 5:6],
                                   gmlp_b[:, hf * 512:(hf + 1) * 512], ALU.mult, ALU.mult),
                            [bt, stp_t[ii], gmlp_t], [vn_t[ii]])
                    else:
                        vs32 = tmpA[0:NS, 0:1024]
                        DVE(stt_fn(vs32[:, hf * 512:(hf + 1) * 512], bk[0:NS, :], stp[0:NS, ii, 5:6],
                                   gmlp_b[0:NS, hf * 512:(hf + 1) * 512], ALU.mult, ALU.mult),
                            [bt, stp_t[ii], gmlp_t], [tmpA_t])
                if ii == 4:
                    vs32 = tmpA[0:NS, 0:1024]
                    P.dma('sp', dma_fn(vrow_s[l, :, :], vs32), [tmpA_t], [])
                    DVE(cp_fn(vn_s[0:NS, :], vs32), [tmpA_t], [vns_t])
                    bk, bt = bank()
                    for g in range(8):
                        MM(bk[:, g * NS:(g + 1) * NS], vn_s[0:NS, g * 128:(g + 1) * 128], I_b[0:NS, 0:NS],
                           True, True, [vns_t, cst_t], [bt])
                    for g in range(8):
                        DVE(ts_fn(stm[:, 3, g * NS:(g + 1) * NS], bk[:, g * NS:(g + 1) * NS],
                                  smallp[:, 24 + g:25 + g], smallp[:, 32 + g:33 + g], ALU.mult, ALU.add),
                            [bt, smallp_t], [stm_t[3]])

            ck(2)
            for gh in range(2):
                wu = load_w([(win_v, C_U + gh * 512, 512, 0)])
                wz = load_w([(win_v, C_Z + gh * 512, 512, 0)])
                for g4 in range(4):
                    g = gh * 4 + g4
                    LD(bsp_b[:, g % 2, :], b_spatial[l, g].partition_broadcast(128), [], [bsp2_t[g % 2]])
                    bu, but = bank()
                    bz, bzt = bank()
                    bm, bmt = bank()
                    for kc in range(KC):
                        MM(bu[:, :], wu[0][:, kc, g4 * 128:(g4 + 1) * 128], hT[:, kc, 0:TB], kc == 0, kc == KC - 1,
                           hT_t[0:4] + wu[1], [but])
                    for kc in range(KC):
                        MM(bz[:, :], wz[0][:, kc, g4 * 128:(g4 + 1) * 128], hT[:, kc, 0:TB], kc == 0, kc == KC - 1,
                           hT_t[0:4] + wz[1], [bzt])
                    for i in range(4):
                        MM(bm[:, i * 128:(i + 1) * 128], vn[:, i, g * 128:(g + 1) * 128], wsT[:, g, :], True, True,
                           [vn_t[i], wsT_t], [bmt])
                    t_sz = tmpA[:, 0:TB]
                    t_1 = tmpA[:, TOK:TOK + TB]
                    ACT(act_fn(t_sz, bz[:, :], AF.Silu), [bzt], [tmpA_t])
                    DVE(tt_fn(t_1.rearrange("p (i t) -> p i t", i=4), bm[:].rearrange("p (i t) -> p i t", i=4),
                              bsp_b[:, g % 2, :].unsqueeze(1).to_broadcast([128, 4, 128]), ALU.add),
                        [bmt, bsp2_t[g % 2]], [tmpA_t])
                    DVE(tt_fn(t_1, t_1, t_sz, ALU.mult), [tmpA_t], [tmpA_t])
                    DVE(tt_fn(yaT[:, g, 0:TB], t_1, bu[:, :], ALU.mult), [tmpA_t, but], [ya_t[g]])
                    if has_s:
                        bs, bst = bank()
                        for kc in range(KC):
                            MM(bs[:, 0:NS], wu[0][:, kc, g4 * 128:(g4 + 1) * 128], hT[:, kc, TB:TOK], kc == 0,
                               kc == KC - 1, [hT_t[4]] + wu[1], [bst])
                        for kc in range(KC):
                            MM(bs[:, NS:2 * NS], wz[0][:, kc, g4 * 128:(g4 + 1) * 128], hT[:, kc, TB:TOK], kc == 0,
                               kc == KC - 1, [hT_t[4]] + wz[1], [bst])
                        ACT(act_fn(stt[:, 0, :], bs[:, NS:2 * NS], AF.Silu), [bst], [stt_t[0]])
                        DVE(tt_fn(stt[:, 0, :], stt[:, 0, :], stm[:, 3, g * NS:(g + 1) * NS], ALU.mult),
                            [stt_t[0], stm_t[3]], [stt_t[0]])
                        DVE(tt_fn(yaT[:, g, TB:TOK], stt[:, 0, :], bs[:, 0:NS], ALU.mult), [stt_t[0], bst], [ya_t[8]])

            ck(3)
            wab = load_w([(win_v, C_A, 16, 0)])
            for (ti, rows, c0, ii) in tiles:
                ck(3.01 + 0.01 * ii)
                bk, bt = bank()
                for kc in range(KC):
                    MM(bk[0:rows, 0:16], hT[:, kc, c0:c0 + rows], wab[0][:, kc, 0:16], kc == 0, kc == KC - 1,
                       [hT_t[ii]] + wab[1], [bt])
                ACT(act_fn(scal[0:rows, ii, 0:8], bk[0:rows, 8:16], AF.Sigmoid), [bt], [scal_t[ii]])
                DVE(tt_fn(scal[0:rows, 5 + ii, 0:8], bk[0:rows, 0:8], smallp[0:rows, 8:16], ALU.add),
                    [bt, smallp_t, scal_t[ii]], [scal_t[5 + ii]])
            for (ti, rows, c0, ii) in tiles:
                ACT(act_fn(scal[0:rows, 5 + ii, 0:8], scal[0:rows, 5 + ii, 0:8], AF.Exp), [scal_t[5 + ii]],
                    [scal_t[5 + ii]])
            for (ti, rows, c0, ii) in tiles:
                ACT(act_fn(scal[0:rows, 5 + ii, 0:8], scal[0:rows, 5 + ii, 0:8], AF.Ln, bias=ONES_f[0:rows, 0:1], scale=1.0),
                    [scal_t[5 + ii], cst_t], [scal_t[5 + ii]])
                DVE(tt_fn(scal[0:rows, 5 + ii, 0:8], scal[0:rows, 5 + ii, 0:8], smallp[0:rows, 16:24], ALU.mult),
                    [scal_t[5 + ii], smallp_t], [scal_t[5 + ii]])
            for (ti, rows, c0, ii) in tiles:
                ck(3.015 + 0.01 * ii)
                if ii < 4:
                    bk2, bt2 = bank()
                    MM(bk2[:, 0:8], U_f, scal[:, 5 + ii, 0:8], True, True, [cst_t, scal_t[5 + ii]], [bt2])
                    MM(bk2[:, 8:16], ONES_f, scal[:, 5 + ii, 0:8], True, True, [cst_t, scal_t[5 + ii]], [bt2])
                    DVE(cp_fn(scal[:, 10 + ii, 0:16], bk2[:, 0:16]), [bt2], [scal_t[10 + ii]])
                    gc8, gl8 = scal[:, 10 + ii, 0:8], scal[:, 10 + ii, 8:16]
                    sA, sB = [scal_t[10 + ii]], [sc2_t[ii]]
                    ACT(act_fn(sc2[:, ii, 0, :], gc8, AF.Exp), sA, sB)
                    DVE(tt_fn(sc2[:, ii, 1, :], gl8, gc8, ALU.subtract), sA, sB)
                    ACT(act_fn(sc2[:, ii, 1, :], sc2[:, ii, 1, :], AF.Exp), sB, sB)
                    ACT(act_fn(sc2[:, ii, 2, :], gl8, AF.Exp), sA, sB)
                    DVE(ts_fn(sc2[:, ii, 3, :], sc2[:, ii, 0, :], -1.0, None, ALU.mult), sB, sB)
                    DVE(ts_fn(sc2[:, ii, 4, :], gc8, -1.0, None, ALU.mult), sA, sB)
                else:
                    ACT(act_fn(scal[0:NS, 5 + ii, 0:8], scal[0:NS, 5 + ii, 0:8], AF.Exp), [scal_t[5 + ii]], [scal_t[5 + ii]])
                    for which, srcsl in ((0, scal[0:NS, ii, 0:8]), (1, scal[0:NS, 5 + ii, 0:8])):
                        DVE(lambda e, which=which: e.memset(stm[:, which, :], 0.0), [], [stm_t[which]])
                        for s2 in range(NS):
                            DVE(ts_fn(stm[0:NS, which, s2 * 8:(s2 + 1) * 8], srcsl, I_f[0:NS, s2:s2 + 1], None, ALU.mult),
                                [scal_t[ii], scal_t[5 + ii], cst_t], [stm_t[which]])
                    ck(3.07)
                    bk2, bt2 = bank()
                    MM(bk2[:, 0:128], ONES_f, stm[:, 0, :], True, True, [cst_t, stm_t[0]], [bt2])
                    MM(bk2[:, 128:256], ONES_f, stm[:, 1, :], True, True, [cst_t, stm_t[1]], [bt2])
                    ck(3.08)
                    DVE(cp_fn(stm[:, 0, :], bk2[:, 0:128]), [bt2], [stm_t[0]])
                    DVE(cp_fn(stm[:, 1, :], bk2[:, 128:256]), [bt2], [stm_t[1]])

            ck(3.1)
            def front(h, bs, bs2=None):
                qn, kn, vT, szbT = qn2[:, bs, :], kn2[:, bs, :], vT2[:, bs, :], szb2[:, bs, :]
                qn_t, kn_t, vT_t, szb_t = qn2_t[bs], kn2_t[bs], vT2_t[bs], szb2_t[bs]
                wh = load_w([(win_v, C_Q + h * 128, 128, 0), (win_v, C_K + h * 128, 128, 128),
                             (win_v, C_VV + h * 128, 128, 256), (win_v, C_ZB + h * 128, 128, 384)])
                if has_s:
                    bk, bt = bank()
                    for kc in range(KC):
                        MM(bk[0:NS, 0:384], hT[:, kc, TB:TOK], wh[0][:, kc, 0:384], kc == 0, kc == KC - 1,
                           [hT_t[4]] + wh[1], [bt])
                    ACT(act_fn(cstage[0:NS, :, :].rearrange("p a b -> p (a b)"), bk[0:NS, 0:384], AF.Copy), [bt], [cstage_t])
                    for j in range(3):
                        c0_ = j * 1024 + h * 128
                        P.dma('sp', dma_fn(conv_s[l, :, 2, c0_:c0_ + 128], cstage[0:NS, j, :]), [cstage_t], [])
                for j, (dst, dtok) in enumerate(((qn, qn_t), (kn, kn_t), (vT, vT_t))):
                    ch = j * 8 + h
                    pb = 0
                    bk, bt = bank()
                    for kc in range(KC):
                        MM(bk[:, :], wh[0][:, kc, j * 128:(j + 1) * 128], hT[:, kc, 0:TB], kc == 0, kc == KC - 1,
                           hT_t[0:4] + wh[1], [bt])
                    pj = pre[:, pb, :]
                    ACT(act_fn(pj[:, 3:3 + TB], bk[:, :], AF.Copy), [bt], [pre_t[pb]])
                    DVE(cp_fn(pj[:, 0:3], tails[:, ch, :]), [tails_t], [pre_t[pb]])
                    DVE(cp_fn(tails[:, ch, :], pj[:, TB:TB + 3]), [pre_t[pb]], [tails_t])
                    if blk == nblk - 1:
                        P.dma('sp', dma_fn(conv_p[l, :, ch * 128:(ch + 1) * 128].rearrange("i c -> c i"),
                                           tails[:, ch, :], allow_slow_non_contiguous=True), [tails_t], [])
                    cw = convw[:, 0, ch, :]
                    DVE(ts_fn(acc[:, 0:TB], pj[:, 0:TB], cw[:, 0:1], None, ALU.mult), [pre_t[pb], convw_t], [acc_t])
                    for i in range(1, 4):
                        DVE(stt_fn(acc[:, 0:TB], pj[:, i:i + TB], cw[:, i:i + 1], acc[:, 0:TB], ALU.mult, ALU.add),
                            [pre_t[pb], convw_t, acc_t], [acc_t])
                    if has_s:
                        P.dma('sp', dma_fn(cstage[0:NS, :, :], sc[l, :, :, ch * 128:(ch + 1) * 128]), [], [cstage_t])
                        bs, bst = bank()
                        for i in range(3):
                            MM(bs[:, i * NS:(i + 1) * NS], cstage[:, i, :], I_f[:, 0:NS], True, True,
                               [cstage_t, cst_t], [bst])
                        for kc in range(KC):
                            MM(bs[:, 3 * NS:4 * NS], wh[0][:, kc, j * 128:(j + 1) * 128], hT[:, kc, TB:TOK],
                               kc == 0, kc == KC - 1, [hT_t[4]] + wh[1], [bst])
                        DVE(ts_fn(acc[:, TB:TOK], bs[:, 0:NS], cw[:, 0:1], None, ALU.mult), [bst, convw_t], [acc_t])
                        for i in range(1, 4):
                            DVE(stt_fn(acc[:, TB:TOK], bs[:, i * NS:(i + 1) * NS], cw[:, i:i + 1], acc[:, TB:TOK],
                                       ALU.mult, ALU.add), [bst, convw_t, acc_t], [acc_t])
                    if j < 2:
                        ACT(act_fn(acc[:, 0:ntok], acc[:, 0:ntok], AF.Silu), [acc_t], [acc_t])
                        ACT(act_fn(sq[:, 0:ntok], acc[:, 0:ntok], AF.Square), [acc_t], [sq_t])
                        bk2, bt2 = bank()
                        MM(bk2[:, 0:TB], ONES_b, sq[:, 0:TB], True, True, [cst_t, sq_t], [bt2])
                        rsqrt_col(rb[:, 0:TB], bk2[:, 0:TB], TB, 128, 1.0, [bt2], [rb_t])
                        if has_s:
                            bk3, bt3 = bank()
                            MM(bk3[:, 0:NS], ONES_b, sq[:, TB:TOK], True, True, [cst_t, sq_t], [bt3])
                            rsqrt_col(rb[:, TB:TOK], bk3[:, 0:NS], NS, 128, 1.0, [bt3], [rb_t])
                        sc_ = (128.0 ** -0.5) if j == 0 else 1.0
                        DVE(stt_fn(dst[:, 0:ntok], acc[:, 0:ntok], sc_, rb[:, 0:ntok], ALU.mult, ALU.mult),
                            [acc_t, rb_t], [dtok])
                    else:
                        ACT(act_fn(dst[:, 0:ntok], acc[:, 0:ntok], AF.Silu), [acc_t], [dtok])
                bk, bt = bank()
                for kc in range(KC):
                    MM(bk[:, :], wh[0][:, kc, 384:512], hT[:, kc, 0:TB], kc == 0, kc == KC - 1,
                       hT_t[0:4] + wh[1], [bt])
                ACT(act_fn(szbT[:, 0:TB], bk[:, :], AF.Silu), [bt], [szb_t])
                if has_s:
                    bs, bst = bank()
                    for kc in range(KC):
                        MM(bs[:, 0:NS], wh[0][:, kc, 384:512], hT[:, kc, TB:TOK], kc == 0,
                           kc == KC - 1, [hT_t[4]] + wh[1], [bst])
                    ACT(act_fn(szbT[:, TB:TOK], bs[:, 0:NS], AF.Silu), [bst], [szb_t])

                ck(3.2)
            def deltaA(h, bs3, bs):
                qn, kn, vT, szbT = qn2[:, bs3, :], kn2[:, bs3, :], vT2[:, bs3, :], szb2[:, bs3, :]
                qn_t, kn_t, vT_t, szb_t = qn2_t[bs3], kn2_t[bs3], vT2_t[bs3], szb2_t[bs3]
                c_ = lambda i: slice(i * 128, (i + 1) * 128)
                Qf = lambda i: nmr[:, 0, i, :]
                Lf = lambda i: nmr[:, 1, i, :]
                Mf = lambda i: nmr[:, 2, i, :]
                DT = lambda i: nm[:, 0, i, :]
                VT = lambda i: vt2[:, bs, i, :]
                fl = lambda k: (nm[:, 0, :, :] if k == 3 else nmr[:, k, :, :]).rearrange("p t c -> p (t c)")
                At = lambda i: dlb2[:, bs, i, :]
                kd = lambda i: dlb2[:, bs, 4 + i, :]
                Wt = lambda i: dlb2[:, bs, 8 + i, :]
                dlb_t = dlb2_t[bs]
                nQ, nL, nM, nD, nV = nm_t[0:4], nm_t[4:8], nm_t[8:12], nm_t[12:16], vt2_t[bs]
                bg, bgt = bank()
                for i in range(4):
                    DVE(ts_fn(DT(i), U_f, scal[:, 5 + i, h:h + 1], None, ALU.mult), [cst_t, scal_t[5 + i]], [nD[i]])
                    MM(bg[:, c_(i)], ONES_f, DT(i), True, True, [cst_t, nD[i]], [bgt])
                for i in range(4):
                    DVE(stt_fn(DT(i), bg[:, c_(i)], sc2[:, i, 4, h:h + 1], NM_f, ALU.add, ALU.add),
                        [bgt, cst_t, sc2_t[i]], [nD[i]])
                ACT(act_fn(fl(3), fl(3), AF.Exp), nD, nD)
                bkk, bkkt = bank()
                bat, batt = bank()
                for i in range(4):
                    MM(bkk[:, c_(i)], kn[:, c_(i)], kn[:, c_(i)], True, True, [kn_t], [bkkt])
                for i in range(4):
                    MM(bat[:, c_(i)], kn[:, c_(i)], qn[:, c_(i)], True, True, [kn_t, qn_t], [batt])
                for i in range(4):
                    DVE(stt_fn(Qf(i), bkk[:, c_(i)], scal[:, i, h:h + 1], DT(i), ALU.mult, ALU.mult),
                        [bkkt, nD[i], scal_t[i]], [nQ[i]])
                DVE(tt_fn(dlb2[:, bs, 0:4, :].rearrange("p t c -> p (t c)"), bat[:, :], fl(3), ALU.mult), [batt] + nD,
                    dlb_t[0:4])
                DVE(tt_fn(nmr[:, 0, :, :], nmr[:, 0, :, :], SU_f.unsqueeze(1).to_broadcast([128, 4, 128]), ALU.mult),
                    nQ + [cst_t], nQ)
                DVE(stt_fn(nmr[:, 2, :, :], nmr[:, 0, :, :], -1.0, I_f.unsqueeze(1).to_broadcast([128, 4, 128]),
                           ALU.mult, ALU.add), nQ + [cst_t], nM)
                bl, blt = bank()
                for i in range(4):
                    MM(bl[:, c_(i)], Qf(i), idr[:], True, True, [nQ[i], cst_t], [blt])
                ACT(act_fn(fl(1), bl[:, :], AF.Copy), [blt], nL)
                for lev in range(1, 7):
                    b1, b1t = bank()
                    for i in range(4):
                        MM(b1[:, c_(i)], Qf(i), Lf(i), True, True, [nQ[i], nL[i]], [b1t])
                    if lev < 6:
                        b3, b3t = bank()
                        for i in range(4):
                            MM(b3[:, c_(i)], Lf(i), Qf(i), True, True, [nQ[i], nL[i]], [b3t])
                    ACT(act_fn(fl(1), b1[:, :], AF.Copy), [b1t], nL)
                    if lev < 6:
                        DVE(cp_fn(fl(0), b3[:, :]), [b3t], nQ)
                    b2, b2t = bank()
                    for i in range(4):
                        MM(b2[:, c_(i)], Lf(i), Mf(i), True, True, [nL[i], nM[i]], [b2t])
                    if lev < 6:
                        DVE(tt_fn(fl(2), b2[:, :], fl(2), ALU.add), [b2t] + nM, nM)
                    else:
                        DVE(tt_fn(dlb2[:, bs, 8:12, :].rearrange("p t c -> p (t c)"), b2[:, :], fl(2), ALU.add),
                            [b2t] + nM, dlb_t[8:12])
                btk, btkt = bank()
                btv, btvt = bank()
                for i in range(4):
                    MM(btk[:, c_(i)], kn[:, c_(i)], I_b, True, True, [kn_t, cst_t], [btkt])
                for i in range(4):
                    MM(btv[:, c_(i)], vT[:, c_(i)], I_b, True, True, [vT_t, cst_t], [btvt])
                for i in range(4):
                    ACT(act_fn(kd(i), btk[:, c_(i)], AF.Copy, scale=sc2[:, i, 1, h:h + 1]), [btkt, sc2_t[i]],
                        [dlb_t[4 + i]])
                ACT(act_fn(vt2[:, bs, :, :].rearrange("p t c -> p (t c)"), btv[:, :], AF.Copy), [btvt], nV)
            def deltaB(h, bs3, bs):
                qn, kn, vT, szbT = qn2[:, bs3, :], kn2[:, bs3, :], vT2[:, bs3, :], szb2[:, bs3, :]
                qn_t, kn_t, vT_t, szb_t = qn2_t[bs3], kn2_t[bs3], vT2_t[bs3], szb2_t[bs3]
                c_ = lambda i: slice(i * 128, (i + 1) * 128)
                VT = lambda i: vt2[:, bs, i, :]
                At = lambda i: dlb2[:, bs, i, :]
                kd = lambda i: dlb2[:, bs, 4 + i, :]
                Wt = lambda i: dlb2[:, bs, 8 + i, :]
                dlb_t = dlb2_t[bs]
                nV = vt2_t[bs]
                for i in range(4):
                    cs = c_(i)
                    beta = scal[:, i, h:h + 1]
                    eg, egl, negeg = sc2[:, i, 0, h:h + 1], sc2[:, i, 2, h:h + 1], sc2[:, i, 3, h:h + 1]
                    bp, bpt = bank()
                    MM(bp[:, 0:128], kn[:, cs], S_bf[:, h, :], True, True, [kn_t, Sb_t[h]], [bpt])
                    bp2, bp2t = bank()
                    MM(bp2[:, 0:128], qn[:, cs], S_bf[:, h, :], True, True, [qn_t, Sb_t[h]], [bp2t])
                    Z, Zt = dlc[:, 0, :], dlc_t[0]
                    DVE(stt_fn(Z, bp[:, 0:128], negeg, VT(i), ALU.mult, ALU.add), [bpt, nV[i], sc2_t[i]], [Zt])
                    bv, bvt = bank()
                    MM(bv[:, 0:128], Wt(i), Z, True, True, [dlb_t[8 + i], Zt], [bvt])
                    vnw, vnwt = dlc[:, 1, :], dlc_t[1]
                    ACT(act_fn(vnw, bv[:, 0:128], AF.Copy, scale=beta), [bvt, scal_t[i]], [vnwt])
                    ACT(act_fn(dl[:, 0, :], bp2[:, 0:128], AF.Copy, scale=eg), [bp2t, sc2_t[i]], [dl_t[0]])
                    bo, bot = bank()
                    MM(bo[:, 0:128], At(i), vnw, True, True, [dlb_t[i], vnwt], [bot])
                    MM(bo[:, 128:256], kd(i), vnw, True, True, [dlb_t[4 + i], vnwt], [bot])
                    DVE(stt_fn(S_bf[:, h, :], S[:, h, :], egl, bo[:, 128:256], ALU.mult, ALU.add),
                        [S_t[h], bot, sc2_t[i]], [Sb_t[h]])
                    DVE(stt_fn(S[:, h, :], S[:, h, :], egl, bo[:, 128:256], ALU.mult, ALU.add),
                        [S_t[h], bot, sc2_t[i]], [S_t[h]])
                    DVE(tt_fn(dl[:, 0, :], dl[:, 0, :], bo[:, 0:128], ALU.add), [dl_t[0], bot], [dl_t[0]])
                    ACT(act_fn(dl[:, 1, :], dl[:, 0, :], AF.Square, accum_out=st0[:, 2:3]), [dl_t[0]], [dl_t[1], st0_t])
                    rsqrt_col(st0[:, 3:4], st0[:, 2:3], 1, 128, 1.0 / 128, [st0_t], [st0_t])
                    yb, ybt_ = dlc[:, 2, :], dlc_t[2]
                    DVE(stt_fn(yb, dl[:, 0, :], st0[:, 3:4], gdn_b[:, :], ALU.mult, ALU.mult),
                        [dl_t[0], st0_t, gdn_t], [ybt_])
                    by, byt = bank()
                    MM(by[:, 0:128], yb, I_b, True, True, [ybt_, cst_t], [byt])
                    DVE(tt_fn(ybT[:, h, cs], by[:, 0:128], szbT[:, cs], ALU.mult), [byt, szb_t], [yb_t[h]])
                ck(3.3)
                if blk == nblk - 1:
                    P.dma('sp', dma_fn(gdn_p[l, h, :, :], S[:, h, :]), [S_t[h]], [])

            def deltaS(h, bs3):
                qn, kn, vT, szbT = qn2[:, bs3, :], kn2[:, bs3, :], vT2[:, bs3, :], szb2[:, bs3, :]
                qn_t, kn_t, vT_t, szb_t = qn2_t[bs3], kn2_t[bs3], vT2_t[bs3], szb2_t[bs3]
                if has_s:
                    ss = slice(TB, TOK)
                    beta_b = stm[:, 0, :].rearrange("p (s h) -> p s h", s=NS)[:, :, h]
                    eg_b = stm[:, 1, :].rearrange("p (s h) -> p s h", s=NS)[:, :, h]
                    kq = dl[:, 2, 0:2 * NS].rearrange("p (s two) -> p s two", two=2)
                    DVE(cp_fn(kq[:, :, 0], kn[:, ss]), [kn_t], [dl_t[2]])
                    DVE(cp_fn(kq[:, :, 1], qn[:, ss]), [qn_t], [dl_t[2]])
                    bst_, bstt = bank()
                    for s in range(NS):
                        sl = s % 4
                        if s % 2 == 0:
                            P.dma('sp', dma_fn(sst_in[:, sl:sl + 2, :], sg[l, s:s + 2, h, :, :].rearrange("s k v -> k s v")),
                                  [], [ssi_t[sl], ssi_t[sl + 1]])
                        MM(bst_[:, 2 * s:2 * s + 2], sst_in[:, sl, :], kq[:, s, :], True, True, [ssi_t[sl], dl_t[2]],
                           [bstt])
                    stkq = bst_[:, 0:2 * NS].rearrange("p (s two) -> p s two", two=2)
                    DVE(tt_fn(stt[:, 2, :], stkq[:, :, 0], eg_b, ALU.mult), [bstt, stm_t[1]], [stt_t[2]])
                    DVE(tt_fn(stt[:, 2, :], vT[:, ss], stt[:, 2, :], ALU.subtract), [vT_t, stt_t[2]], [stt_t[2]])
                    DVE(tt_fn(stt[:, 2, :], stt[:, 2, :], beta_b, ALU.mult), [stt_t[2], stm_t[0]], [stt_t[2]])
                    DVE(tt_fn(stt[:, 3, :], kq[:, :, 0], kq[:, :, 1], ALU.mult), [dl_t[2]], [stt_t[3]])
                    bqk, bqkt = bank()
                    MM(bqk[:, 0:NS], ONES_f, stt[:, 3, :], True, True, [cst_t, stt_t[3]], [bqkt])
                    DVE(tt_fn(stt[:, 4, :], stkq[:, :, 1], eg_b, ALU.mult), [bstt, stm_t[1]], [stt_t[4]])
                    DVE(tt_fn(stt[:, 3, :], bqk[:, 0:NS], stt[:, 2, :], ALU.mult), [bqkt, stt_t[2]], [stt_t[3]])
                    DVE(tt_fn(stt[:, 4, :], stt[:, 4, :], stt[:, 3, :], ALU.add), [stt_t[3], stt_t[4]], [stt_t[4]])
                    btp, btpt = bank()
                    MM(btp[0:NS, 0:128], stt[:, 2, :], I_f, True, True, [stt_t[2], cst_t], [btpt])
                    MM(btp[0:NS, 128:256], kq[:, :, 0], I_f, True, True, [dl_t[2], cst_t], [btpt])
                    MM(btp[0:NS, 256:384], stt[:, 4, :], I_f, True, True, [stt_t[4], cst_t], [btpt])
                    DVE(cp_fn(stm[0:NS, 2, :], btp[0:NS, 0:128]), [btpt], [stm_t[2]])
                    DVE(cp_fn(spad[0:NS, 0, :], btp[0:NS, 128:256]), [btpt], [spad_t[0]])
                    ACT(act_fn(stmb[0:NS, 1, :], btp[0:NS, 256:384], AF.Square, accum_out=st0[0:NS, 6:7]),
                        [btpt, stm_t[2], spad_t[0]], [stmb_t[1], st0s_t])
                    rsqrt_col(st0[0:NS, 7:8], st0[0:NS, 6:7], 1, NS, 1.0 / 128, [st0s_t], [st0s_t])
                    DVE(stt_fn(stmb[0:NS, 0, :], btp[0:NS, 256:384], st0[0:NS, 7:8], gdn_b[0:NS, :], ALU.mult, ALU.mult),
                        [btpt, st0s_t, gdn_t], [stmb_t[0]])
                    by, byt = bank()
                    MM(by[:, 0:NS], stmb[0:NS, 0, :], I_b[0:NS, 0:NS], True, True, [stmb_t[0], cst_t], [byt])
                    DVE(tt_fn(ybT[:, h, ss], by[:, 0:NS], szbT[:, ss], ALU.mult), [byt, szb_t], [yb_t[8]])
                    msk = [(spad[0:NS, 1, :], spad[0:NS, 1, :], spad_t[1]), (spad[0:NS, 2, :], spad[0:NS, 2, :], spad_t[2])]
                    for s in (0, 2):
                        P.dma('sp', dma_fn(sst_in[:, s:s + 2, :], sg[l, s:s + 2, h, :, :].rearrange("s k v -> k s v")),
                              [], [ssi_t[s], ssi_t[s + 1]])
                    for s0 in range(0, NS, 2):
                        for u in range(2):
                            s = s0 + u
                            DVE(ts_fn(msk[u][1], stm[0:NS, 2, :], I_f[0:NS, s:s + 1], None, ALU.mult),
                                [stm_t[2], cst_t], [msk[u][2]])
                        bop, bopt = bank()
                        for u in range(2):
                            MM(bop[:, u * 128:(u + 1) * 128], spad[0:NS, 0, :], msk[u][0], True, True,
                               [spad_t[0], msk[u][2]], [bopt])
                        for u in range(2):
                            s = s0 + u
                            sl = s % 4
                            DVE(stt_fn(sst_out[:, u, :], sst_in[:, sl, :], eg_b[:, s:s + 1], bop[:, u * 128:(u + 1) * 128],
                                       ALU.mult, ALU.add), [ssi_t[sl], stm_t[1], bopt], [sso_t[u]])
                        P.dma('sp', dma_fn(gdn_s[l, s0:s0 + 2, h, :, :].rearrange("s k v -> k s v"), sst_out[:, 0:2, :]),
                              [sso_t[0], sso_t[1]], [])
                        s = s0 + 4
                        if s < NS:
                            P.dma('sp', dma_fn(sst_in[:, s % 4:s % 4 + 2, :], sg[l, s:s + 2, h, :, :].rearrange("s k v -> k s v")),
                                  [], [ssi_t[s % 4], ssi_t[s % 4 + 1]])

            BF_, BA_, BB_, BS_ = [0, 1], [2, 3], [4, 5], [6, 7]
            QF_, QA_, QB_, QS_ = (4, 4, 3, 3) if has_s else (8, 8, 4, 3)
            front(0, 0)
            co_run(P, [((lambda: deltaA(0, 0, 0)), QA_, BA_), ((lambda: front(1, 1)), QF_, BF_)])
            for h in range(H):
                tasks = [((lambda h=h: deltaB(h, h % 3, h % 2)), QB_, BB_)]
                if has_s:
                    tasks.append(((lambda h=h: deltaS(h, h % 3)), QS_, BS_))
                if h + 1 < H:
                    tasks.append(((lambda h=h: deltaA(h + 1, (h + 1) % 3, (h + 1) % 2)), QA_, BA_))
                if h + 2 < H:
                    tasks.append(((lambda h=h: front(h + 2, (h + 2) % 3)), QF_, BF_))
                co_run(P, tasks)

            ck(4)
            ybr = yb_t[0:9] if has_s else yb_t[0:8]
            yar = ya_t[0:9] if has_s else ya_t[0:8]
            for qd in range(4):
                wa = load_w([(wpa_v, qd * 256, 256, 0), (win_v, C_GA + qd * 256, 256, 256)])
                wb = load_w([(wpb_v, qd * 256, 256, 0), (win_v, C_GB + qd * 256, 256, 256)])
                for j2 in range(2):
                    j = qd * 2 + j2
                    groups = [(0, TB, hT_t[0:4])] + ([(TB, NS, [hT_t[4]])] if has_s else [])
                    for (c0, n, htk) in groups:
                        bpa, bpat = bank()
                        bpb, bpbt = bank()
                        bga, bgat = bank()
                        bgb, bgbt = bank()
                        for kc in range(KC):
                            MM(bpa[:, 0:n], wa[0][:, kc, j2 * 128:(j2 + 1) * 128], yaT[:, kc, c0:c0 + n], kc == 0,
                               kc == KC - 1, yar + wa[1], [bpat])
                        for kc in range(KC):
                            MM(bpb[:, 0:n], wb[0][:, kc, j2 * 128:(j2 + 1) * 128], ybT[:, kc, c0:c0 + n], kc == 0,
                               kc == KC - 1, ybr + wb[1], [bpbt])
                        for kc in range(KC):
                            MM(bga[:, 0:n], wa[0][:, kc, 256 + j2 * 128:256 + (j2 + 1) * 128], hT[:, kc, c0:c0 + n],
                               kc == 0, kc == KC - 1, htk + wa[1], [bgat])
                        for kc in range(KC):
                            MM(bgb[:, 0:n], wb[0][:, kc, 256 + j2 * 128:256 + (j2 + 1) * 128], hT[:, kc, c0:c0 + n],
                               kc == 0, kc == KC - 1, htk + wb[1], [bgbt])
                        ta = tmpA[:, 0:n]
                        tb_ = tmpA[:, TOK:TOK + n]
                        ACT(act_fn(ta, bga[:, 0:n], AF.Sigmoid), [bgat], [tmpA_t])
                        ACT(act_fn(tb_, bgb[:, 0:n], AF.Sigmoid), [bgbt], [tmpA_t])
                        DVE(tt_fn(ta, ta, bpa[:, 0:n], ALU.mult), [tmpA_t, bpat], [tmpA_t])
                        DVE(tt_fn(tb_, tb_, bpb[:, 0:n], ALU.mult), [tmpA_t, bpbt], [tmpA_t])
                        DVE(tt_fn(mg[:, j, c0:c0 + n], ta, tb_, ALU.add), [tmpA_t], [mg_t[j]] if c0 == 0 else [mg_s_t])

            ck(5)
            LD(gslot[:], post_norm[l].partition_broadcast(128), [], [gslot_t])
            wo = [load_w([(wo_v, hf * 512, 512, 0)]) for hf in range(2)]
            for (ti, rows, c0, ii) in tiles:
                bks = []
                for hf in range(2):
                    bk, bt = bank()
                    for kc in range(KC):
                        MM(bk[0:rows, :], mg[:, kc, c0:c0 + rows], wo[hf][0][:, kc, :], kc == 0, kc == KC - 1,
                           (mg_t if ii < 4 else [mg_s_t]) + wo[hf][1], [bt])
                    ACT(act_fn(jh[0:rows, 0:512], bk[0:rows, :], AF.Square, accum_out=stp[0:rows, ii, 2 + hf:3 + hf]),
                        [bt], [jh_t, stp_t[ii]])
                    bks.append((bk, bt))
                DVE(tt_fn(stp[0:rows, ii, 4:5], stp[0:rows, ii, 2:3], stp[0:rows, ii, 3:4], ALU.add), [stp_t[ii]], [stp_t[ii]])
                rsqrt_col(stp[0:rows, ii, 5:6], stp[0:rows, ii, 4:5], 1, rows, 1.0 / D, [stp_t[ii]], [stp_t[ii]])
                for hf in range(2):
                    bk, bt = bks[hf]
                    t1 = tmpA[0:rows, hf * TOK:hf * TOK + 512]
                    DVE(stt_fn(t1, bk[0:rows, :], stp[0:rows, ii, 5:6], gslot[0:rows, hf * 512:(hf + 1) * 512], ALU.mult,
                               ALU.mult), [bt, stp_t[ii], gslot_t], [tmpA_h[hf]])
                    xs = x_tm[0:rows, ti, hf * 512:(hf + 1) * 512]
                    DVE(tt_fn(xs, xs, t1, ALU.add), [tmpA_h[hf], x_t[ti]], [x_t[ti]])
                if l == depth - 1:
                    if ii < 4:
                        P.dma('sp', dma_fn(y_p[ti * 128:(ti + 1) * 128, :], x_tm[:, ti, :]), [x_t[ti]], [])
                    else:
                        P.dma('sp', dma_fn(y_s[:, :], x_tm[0:NS, 16, :]), [x_t[16]], [])

    sems = {}
    for e in ENGS:
        sems[e] = es.enter_context(nc.semaphore("s_" + e))
    for key in P.dma_uses:
        sems[key] = es.enter_context(nc.semaphore("d_%s_%d" % key))
    final_waits = [(k, 16 * u) for k, u in P.dma_uses.items()] + [(e, P.cnt[e]) for e in ENGS if e != 'sp' and P.cnt[e]]
    P.q['sp'].append((final_waits, None, None))

    def replay(name, e):
        for waits, fn, sig in P.q[name]:
            for k, v in waits:
                e.wait_ge(sems[k], v)
            if fn is None:
                continue
            ins = fn(e)
            if sig is None:
                continue
            if isinstance(sig, tuple):
                ins.then_inc(sems[sig], 16)
            else:
                ins.then_inc(sems[sig], 1)

    with nc.Block() as block:
        @block.tensor
        def _(e):
            replay('pe', e)

        @block.scalar
        def _(e):
            replay('act', e)

        @block.vector
        def _(e):
            replay('dve', e)

        @block.gpsimd
        def _(e):
            replay('pool', e)

        @block.sync
        def _(e):
            replay('sp', e)
    es.close()
    return nc


def make_consts():
    c = np.zeros((128, 6, 128), np.float32)
    idx = np.arange(128)
    c[:, 0, :] = np.eye(128)
    c[:, 1, :] = (idx[:, None] <= idx[None, :])
    c[:, 2, :] = (idx[None, :] > idx[:, None])
    c[:, 3, :] = np.where(idx[None, :] >= idx[:, None], 0.0, -1e30)
    c[:, 4, :] = 1.0
    c[:, 5, :] = (idx[None, :] <= idx[:, None])
    return np.ascontiguousarray(c.reshape(128, 768))


def kernel(x_prompt, x_sample, state_gdn, state_conv, pre_norm, w_in, gmlp_norm, w_spatial, b_spatial, conv_w,
           a_log, dt_bias, gdn_norm, w_proj_a, w_proj_b, w_out, post_norm):
    f = lambda a: np.ascontiguousarray(np.asarray(a, dtype=np.float32))
    nc = build()
    consts = make_consts()
    shared = dict(pre_norm=f(pre_norm), w_in=f(w_in), gmlp_norm=f(gmlp_norm), w_spatial=f(w_spatial),
                  b_spatial=f(b_spatial), conv_w=f(conv_w), a_log=f(a_log), dt_bias=f(dt_bias), gdn_norm=f(gdn_norm),
                  w_proj_a=f(w_proj_a), w_proj_b=f(w_proj_b), w_out=f(w_out), post_norm=f(post_norm), consts=consts)
    in_maps = []
    for c in range(8):
        m = dict(shared)
        m["x_p"] = f(x_prompt[c])
        m["x_s"] = f(x_sample[c * NS:(c + 1) * NS, 0, :])
        m["sg"] = f(state_gdn[:, c * NS:(c + 1) * NS])
        m["sc"] = f(state_conv[:, c * NS:(c + 1) * NS])
        in_maps.append(m)
    res = run_bass_kernel_spmd(nc, in_maps, core_ids=list(range(8)))
    r = res.results
    y_prompt = np.stack([r[c]["y_p"] for c in range(8)], axis=0)
    y_sample = np.concatenate([r[c]["y_s"] for c in range(8)], axis=0)[:, None, :]
    gdn_prompt = np.stack([r[c]["gdn_p"] for c in range(8)], axis=1)
    conv_prompt = np.stack([r[c]["conv_p"] for c in range(8)], axis=1)
    gdn_sample = np.concatenate([r[c]["gdn_s"] for c in range(8)], axis=1)
    conv_sample = np.concatenate([r[c]["conv_s"] for c in range(8)], axis=1)
    vrow = np.concatenate([r[c]["vrow_s"] for c in range(8)], axis=1)[:, :, None, :]
    return (y_prompt.astype(np.float32), y_sample.astype(np.float32), gdn_prompt.astype(np.float32),
            conv_prompt.astype(np.float32), gdn_sample.astype(np.float32), conv_sample.astype(np.float32),
            vrow.astype(np.float32))
```
